# Optimizing a Trainium2 kernel written in Bass

```python
import math
import jax, jax.numpy as jnp
from jax import lax
import numpy as np

D_MODEL = 1024
BATCH = 8
SEQ = 4096
DEPTH = 2

ATTN_WIDTH = D_MODEL // 2
ATTN_HEAD_DIM = 64
ATTN_QK_DIM = ATTN_HEAD_DIM // 2
ATTN_HEADS = ATTN_WIDTH // ATTN_HEAD_DIM
POOL_WIDTH = D_MODEL // 4
POOL_WINDOWS = (2, 4, 8, 16)
POOL_GROUPS = len(POOL_WINDOWS)
POOL_GROUP_DIM = POOL_WIDTH // POOL_GROUPS
CONV_WIDTH = D_MODEL // 4
CONV_GROUPS = 4
CONV_K = 3
MIX_WIDTH = ATTN_WIDTH + POOL_WIDTH + CONV_WIDTH
IN_WIDTH = 3 * ATTN_WIDTH + POOL_WIDTH + 3 * CONV_WIDTH
D_FF = ((8 * D_MODEL // 3 + 255) // 256) * 256
Q_BLOCK = 128
EPS = 1e-6

kernel_name = "hymba_style_diffattn_pool_shortconv_encoder"


def rmsnorm(x, g):
    xf = x.astype(jnp.float32)
    y = xf * lax.rsqrt(jnp.mean(xf * xf, axis=-1, keepdims=True) + EPS)
    return (y * g.astype(jnp.float32)).astype(x.dtype)


def alibi_slopes(n_heads):
    return jnp.asarray(np.array([2.0 ** (-8.0 * (h + 1) / n_heads) for h in range(n_heads)], dtype=np.float32))


def diff_attention(q, k, v, lam, slopes):
    b_, s_ = q.shape[0], q.shape[1]
    nblk = s_ // Q_BLOCK
    scale = ATTN_QK_DIM ** -0.5
    qb = q.reshape(b_, nblk, Q_BLOCK, ATTN_HEADS, 2, ATTN_QK_DIM).transpose(1, 0, 2, 3, 4, 5)
    kpos = jnp.arange(s_)

    def one_block(args):
        qi, blk = args
        qpos = blk * Q_BLOCK + jnp.arange(Q_BLOCK)
        dist = jnp.abs(qpos[:, None] - kpos[None, :]).astype(jnp.float32)
        bias = -slopes[:, None, None] * dist[None]
        s = jnp.einsum('bqhpd,bkhpd->bphqk', qi, k).astype(jnp.float32) * scale + bias
        p = jax.nn.softmax(s, axis=-1)
        w = p[:, 0] - lam * p[:, 1]
        return jnp.einsum('bhqk,bkhd->bqhd', w.astype(v.dtype), v)

    out = lax.map(one_block, (qb, jnp.arange(nblk)))
    return out.transpose(1, 0, 2, 3, 4).reshape(b_, s_, ATTN_HEADS, ATTN_HEAD_DIM)


def multiscale_pool(u, w_pool, pool_scale):
    b_, s_, _ = u.shape
    uf = u.astype(jnp.float32)
    csum = jnp.concatenate([jnp.zeros((b_, 1, POOL_WIDTH), jnp.float32), jnp.cumsum(uf, axis=1)], axis=1)
    t = jnp.arange(s_)
    pooled = []
    for g, w in enumerate(POOL_WINDOWS):
        cg = csum[..., g * POOL_GROUP_DIM:(g + 1) * POOL_GROUP_DIM]
        lo = jnp.clip(t - w // 2, 0, s_)
        hi = jnp.clip(t + w - w // 2, 0, s_)
        total = jnp.take(cg, hi, axis=1) - jnp.take(cg, lo, axis=1)
        cnt = (hi - lo).astype(jnp.float32)
        pooled.append(total / cnt[None, :, None])
    pooled = jnp.stack(pooled, axis=2)
    diff = pooled - uf.reshape(b_, s_, POOL_GROUPS, POOL_GROUP_DIM)
    mixed = jnp.einsum('bsgc,gcd->bsgd', diff, w_pool.astype(jnp.float32))
    return (mixed.reshape(b_, s_, POOL_WIDTH) * pool_scale.astype(jnp.float32)).astype(u.dtype)


def short_gated_conv(b_gate, c_gate, xh, conv_w):
    u = c_gate * xh
    up = jnp.pad(u, ((0, 0), (1, 1), (0, 0)))
    y = conv_w[0] * up[:, :-2] + conv_w[1] * up[:, 1:-1] + conv_w[2] * up[:, 2:]
    return b_gate * y


def modulate(h, shift, scale):
    return h * (1.0 + scale[:, None, :]) + shift[:, None, :]


def setup_inputs(seed: int = 0) -> dict:
    key = jax.random.key(seed)
    ks = jax.random.split(key, 20)
    f32 = jnp.float32
    n = lambda k, shape, s: jax.random.normal(k, shape, f32) * s
    return {
        "x": n(ks[0], (BATCH, SEQ, D_MODEL), 1.0),
        "c": n(ks[1], (BATCH, D_MODEL), 1.0),
        "w_ada": n(ks[2], (DEPTH, D_MODEL, 6 * D_MODEL), D_MODEL ** -0.5),
        "b_ada": n(ks[3], (DEPTH, 6 * D_MODEL), 0.02),
        "g_mix": 1.0 + n(ks[4], (DEPTH, D_MODEL), 0.1),
        "w_in": n(ks[5], (DEPTH, D_MODEL, IN_WIDTH), D_MODEL ** -0.5),
        "lambda_q1": n(ks[6], (DEPTH, ATTN_QK_DIM), 0.1),
        "lambda_k1": n(ks[7], (DEPTH, ATTN_QK_DIM), 0.1),
        "lambda_q2": n(ks[8], (DEPTH, ATTN_QK_DIM), 0.1),
        "lambda_k2": n(ks[9], (DEPTH, ATTN_QK_DIM), 0.1),
        "g_subln": 1.0 + n(ks[10], (DEPTH, ATTN_WIDTH), 0.1),
        "w_pool": n(ks[11], (DEPTH, POOL_GROUPS, POOL_GROUP_DIM, POOL_GROUP_DIM), POOL_GROUP_DIM ** -0.5),
        "pool_scale": 1.0 + n(ks[12], (DEPTH, POOL_WIDTH), 0.1),
        "conv_w": n(ks[13], (DEPTH, CONV_K, CONV_WIDTH), 0.5),
        "w_out": n(ks[14], (DEPTH, MIX_WIDTH, D_MODEL), MIX_WIDTH ** -0.5),
        "g_ffn": 1.0 + n(ks[15], (DEPTH, D_MODEL), 0.1),
        "w_gate_up": n(ks[16], (DEPTH, D_MODEL, 2 * D_FF), D_MODEL ** -0.5),
        "w_down": n(ks[17], (DEPTH, D_FF, D_MODEL), D_FF ** -0.5),
        "g_final": 1.0 + n(ks[18], (D_MODEL,), 0.1),
    }


def reference(x, c, w_ada, b_ada, g_mix, w_in, lambda_q1, lambda_k1, lambda_q2, lambda_k2,
              g_subln, w_pool, pool_scale, conv_w, w_out, g_ffn, w_gate_up, w_down, g_final):
    b_, s_, _ = x.shape
    slopes = alibi_slopes(ATTN_HEADS)
    c_act = jax.nn.silu(c)
    o_q, o_k, o_v = ATTN_WIDTH, 2 * ATTN_WIDTH, 3 * ATTN_WIDTH
    o_p = o_v + POOL_WIDTH
    o_b, o_c = o_p + CONV_WIDTH, o_p + 2 * CONV_WIDTH
    for l in range(DEPTH):
        mod = c_act @ w_ada[l] + b_ada[l]
        sh1, sc1, gt1, sh2, sc2, gt2 = jnp.split(mod, 6, axis=-1)

        h = modulate(rmsnorm(x, g_mix[l]), sh1, sc1)
        proj = h @ w_in[l]
        q = proj[..., :o_q].reshape(b_, s_, ATTN_HEADS, 2, ATTN_QK_DIM)
        k = proj[..., o_q:o_k].reshape(b_, s_, ATTN_HEADS, 2, ATTN_QK_DIM)
        v = proj[..., o_k:o_v].reshape(b_, s_, ATTN_HEADS, ATTN_HEAD_DIM)
        u_pool = proj[..., o_v:o_p]
        b_gate, c_gate, x_conv = proj[..., o_p:o_b], proj[..., o_b:o_c], proj[..., o_c:]

        lambda_init = 0.8 - 0.6 * math.exp(-0.3 * l)
        lam = (jnp.exp(jnp.sum(lambda_q1[l].astype(jnp.float32) * lambda_k1[l].astype(jnp.float32)))
               - jnp.exp(jnp.sum(lambda_q2[l].astype(jnp.float32) * lambda_k2[l].astype(jnp.float32)))
               + lambda_init)
        attn = diff_attention(q, k, v, lam, slopes)
        attn = rmsnorm(attn, g_subln[l].reshape(ATTN_HEADS, ATTN_HEAD_DIM)) * (1.0 - lambda_init)
        y_a = attn.reshape(b_, s_, ATTN_WIDTH)
        y_b = multiscale_pool(u_pool, w_pool[l], pool_scale[l])
        y_c = short_gated_conv(b_gate, c_gate, x_conv, conv_w[l])

        mixed = jnp.concatenate([y_a, y_b, y_c], axis=-1) @ w_out[l]
        x = x + gt1[:, None, :] * mixed

        h = modulate(rmsnorm(x, g_ffn[l]), sh2, sc2)
        gate, up = jnp.split(h @ w_gate_up[l], 2, axis=-1)
        x = x + gt2[:, None, :] * ((jax.nn.silu(gate) * up) @ w_down[l])

    return rmsnorm(x, g_final)
```

```python
import math
import numpy as np
import ml_dtypes
import concourse.bass as bass
import concourse.mybir as mybir
from concourse.bass_utils import run_bass_kernel_spmd
from contextlib import ExitStack

F32 = mybir.dt.float32
BF16 = mybir.dt.bfloat16
AF = mybir.ActivationFunctionType
ALU = mybir.AluOpType
AX = mybir.AxisListType

D = 1024
NH = 8
DFF = 2816
INW = 2560
DEPTH = 2
EPS = 1e-6
EPOCH = 12000
EPI_GAP = 2
SKIP_BIAS = 150.0
DVE_BIAS_MOD = 10 ** 9
N_DUMMY = 1
SBUF_BASE = 16512
SBUF_LIMIT = 229376


class Op:
    __slots__ = ("eng", "fn", "deps", "is_dma", "semkey", "sig", "val", "semid")

    def __init__(self, eng, fn, is_dma, semkey):
        self.eng = eng
        self.fn = fn
        self.deps = []
        self.is_dma = is_dma
        self.semkey = semkey
        self.sig = False
        self.val = None
        self.semid = None


class Prog:
    ENGS = ("pe", "act", "dve", "pool", "sp")

    def __init__(self, nc):
        self.nc = nc
        self.ops = {e: [] for e in self.ENGS}
        self.last_writer = {}
        self.readers = {}
        self.last_compute = {}
        self.last_dma = {}
        self.bar_deps = []
        self.bar_gen = 0
        self.eng_gen = {e: 0 for e in self.ENGS}

    def add(self, eng, method, reads=(), writes=(), dma=False, semkey=None, **kw):
        def fn(e, method=method, kw=kw):
            return getattr(e, method)(**kw)
        op = Op(eng, fn, dma, semkey if dma else None)
        if dma and semkey is None:
            op.semkey = ("dmaq", eng)
        deps = {}
        for b in reads:
            w = self.last_writer.get(b)
            if w is not None:
                deps[id(w)] = (w, "RAW")
        for b in writes:
            w = self.last_writer.get(b)
            if w is not None and id(w) not in deps:
                deps[id(w)] = (w, "WAW")
            for r in self.readers.get(b, ()):
                if id(r) not in deps:
                    deps[id(r)] = (r, "WAR")
        if self.eng_gen[eng] < self.bar_gen:
            self.eng_gen[eng] = self.bar_gen
            for d in self.bar_deps:
                deps[id(d)] = (d, "BAR")
        for d, kind in deps.values():
            if d is op:
                continue
            same = (d.eng == eng) and (not d.is_dma) and (not dma)
            if same:
                if eng == "pe" or kind in ("WAR", "BAR"):
                    continue
            op.deps.append(d)
            d.sig = True
        for b in reads:
            self.readers.setdefault(b, []).append(op)
        for b in writes:
            self.last_writer[b] = op
            self.readers[b] = []
        self.ops[eng].append(op)
        if dma:
            self.last_dma[op.semkey] = op
        else:
            self.last_compute[eng] = op
        return op

    def barrier(self):
        self.bar_deps = list(self.last_compute.values()) + list(self.last_dma.values())
        self.bar_gen += 1

    def emit(self):
        nc = self.nc
        finals = list(self.last_compute.values()) + list(self.last_dma.values())
        for d in finals:
            d.sig = True
        counters = {}
        semids = {}
        for e in self.ENGS:
            for op in self.ops[e]:
                if not op.sig:
                    continue
                key = op.semkey if op.is_dma else ("eng", e)
                c = counters.get(key, 0)
                counters[key] = c + 1
                op.semid = (key, c // EPOCH)
                op.val = (c % EPOCH + 1) * (16 if op.is_dma else 1)
                semids[op.semid] = None
        with ExitStack() as st:
            sems = {}
            for i, k in enumerate(semids):
                sems[k] = st.enter_context(nc.semaphore("s%d" % i))
            self.n_sems = len(sems)
            block = st.enter_context(nc.Block())
            engmap = {"pe": block.tensor, "act": block.scalar, "dve": block.vector,
                      "pool": block.gpsimd, "sp": block.sync}

            def make(e):
                def body(eng):
                    waited = {}

                    def wait(d):
                        if waited.get(d.semid, 0) >= d.val:
                            return
                        eng.wait_ge(sems[d.semid], d.val)
                        waited[d.semid] = d.val

                    for op in self.ops[e]:
                        for d in op.deps:
                            wait(d)
                        ins = op.fn(eng)
                        if op.sig:
                            ins.then_inc(sems[op.semid], 16 if op.is_dma else 1)
                    if e == "sp":
                        for d in finals:
                            if d.eng == "sp" and not d.is_dma:
                                continue
                            wait(d)
                return body

            for e in self.ENGS:
                engmap[e](make(e))


class Alloc:
    def __init__(self, nc, base, limit=SBUF_LIMIT):
        self.nc = nc
        self.off = base
        self.limit = limit
        self.n = 0

    def t(self, name, shape, dtype):
        nb = int(np.prod(shape[1:])) * mybir.dt.size(dtype)
        nb = (nb + 31) // 32 * 32
        self.n += 1
        h = self.nc.alloc_sbuf_tensor_at("%s_%d_%d" % (name, self.off, self.n), list(shape), dtype, offset=self.off)
        self.off += nb
        assert self.off <= self.limit, ("SBUF overflow", name, self.off)
        return h


def slopes():
    return [2.0 ** (-(h + 1)) for h in range(NH)]


def build(S):
    NT = S // 128
    NS = S // 512
    NB = 2 * NT
    SP = S + 16
    nc = bass.Bass("TRN2", target_bir_lowering=False)

    def din(name, shape, dtype=F32):
        return nc.dram_tensor(name, list(shape), dtype, kind="ExternalInput").ap()

    x_in = din("x", [S, D])
    cT = din("cT", [128, 8])
    w_ada = din("w_ada", [DEPTH, D, 6 * D])
    b_ada = din("b_ada", [DEPTH, 6 * D])
    gcols = din("gcols", [128, DEPTH * 16])
    w_in = din("w_in", [DEPTH, D, INW])
    lam_in = din("lam", [DEPTH, 128])
    gsub = din("gsub", [64, DEPTH * 8])
    w_pool = din("w_pool", [DEPTH, 4, 64, 64])
    pcols = din("pcols", [128, DEPTH * 8])
    w_out = din("w_out", [DEPTH, D, D])
    w_gu = din("w_gate_up", [DEPTH, D, 2 * DFF])
    w_down = din("w_down", [DEPTH, DFF, D])
    gfin = din("gfin", [1, D])
    c_ident = din("c_ident", [128, 128])
    c_qrel = din("c_qrel", [128, 512], BF16)
    c_alL = din("c_alL", [128, NH * 128], BF16)
    c_alR = din("c_alR", [128, NH * 128], BF16)
    c_bt = din("c_bt", [128, NH * NB])
    c_dt = din("c_dt", [128, 4 * 512])
    c_pool = din("c_pool", [128, 2 * 17])
    c_qr = din("c_qr", [128, 512])
    out = nc.dram_tensor("out", [S, D], F32, kind="ExternalOutput").ap()
    pc = nc.dram_tensor("pc_scr", [8, 128, SP], F32, kind="Internal").ap()
    yts = nc.dram_tensor("yt_scr", [D, S], BF16, kind="Internal").ap()
    wgu_bf = nc.dram_tensor("wgu_bf", [D, 2 * DFF], BF16, kind="Internal").ap()
    wd_bf = nc.dram_tensor("wd_bf", [DFF, D], BF16, kind="Internal").ap()

    P = Prog(nc)
    op = P.add
    ps_all = nc.alloc_psum_tensor("ps_all", [128, 4096], F32)

    def bank(b, n=1):
        return ps_all[:, b * 512:(b + n) * 512]

    def bk(b, n=1):
        return [("ps", b + i) for i in range(n)]

    def ld(dst, src, key, wkeys, rkeys=(), q="sp"):
        return op(q, "dma_start", reads=list(rkeys), writes=list(wkeys), dma=True, semkey=key, out=dst, in_=src)

    def st(dst, src, key, rkeys, wkeys, q="pool"):
        return op(q, "dma_start", reads=list(rkeys), writes=list(wkeys), dma=True, semkey=key, out=dst, in_=src)

    A0 = Alloc(nc, SBUF_BASE)
    ident = A0.t("ident", [128, 128], F32)
    ones32 = A0.t("ones32", [128, 64], F32)
    sel = A0.t("sel", [128, 64], F32)
    poolc = A0.t("poolc", [128, 2, 17], F32)
    gcol_t = A0.t("gcol", [128, DEPTH * 16], F32)
    pcol_t = A0.t("pcol", [128, DEPTH * 8], F32)
    gsub_t = A0.t("gsub", [64, DEPTH * 8], F32)
    cst = A0.t("cst", [128, 8], F32)
    zpad = A0.t("zpad", [128, 8], F32)
    ca = A0.t("ca", [128, 8], F32)
    GT = A0.t("GT", [128, 2, D], F32)
    mcols = A0.t("mcols", [128, 4, 8], F32)
    lamt = A0.t("lamt", [128, 8], F32)
    BASE = A0.off

    ld(ident[:], c_ident, "c0", ["ident"])
    ld(poolc[:], c_pool.rearrange("p (c k) -> p c k", c=2), "c6", ["poolc"])
    ld(gcol_t[:], gcols, "c7", ["gcol"])
    ld(pcol_t[:], pcols, "c8", ["pcol"])
    ld(gsub_t[:], gsub, "c9", ["gsubraw"])
    ld(ca[:], cT, "c10", ["ca"])
    op("pool", "memset", writes=["ones32"], ap=ones32[:], constant=1.0)
    op("pool", "memset", writes=["sel"], ap=sel[:], constant=0.0)
    op("pool", "memset", writes=["sel"], ap=sel[64:65, :], constant=1.0)
    op("pool", "memset", writes=["zpad"], ap=zpad[:], constant=0.0)
    op("pool", "memset", writes=["cst0"], ap=cst[:, 0:1], constant=EPS)
    op("pool", "memset", writes=["cst2"], ap=cst[:, 2:3], constant=0.0)
    for ch in range(8):
        st(pc[ch, :, 0:8], zpad[:], ("zp", 0), ["zpad"], [("pcpad", ch)])
        st(pc[ch, :, S + 8:S + 16], zpad[:], ("zp", 1), ["zpad"], [("pcpad", ch)])
    op("act", "activation", reads=["ca"], writes=["ca"], out=ca[:], in_=ca[:], func=AF.Silu)

    def rstd_ops(ss_ap, rstd_ap, key_ss, key_rstd, scale, npart=128, extra_bias=None):
        op("act", "activation", reads=[key_ss, "cst0"], writes=[key_rstd],
           out=rstd_ap, in_=ss_ap, func=AF.Ln, scale=scale, bias=cst[0:npart, 0:1])
        if extra_bias is None:
            op("act", "activation", reads=[key_rstd], writes=[key_rstd], out=rstd_ap, in_=rstd_ap, func=AF.Exp, scale=-0.5)
        else:
            op("act", "activation", reads=[key_rstd, "cst1"], writes=[key_rstd],
               out=rstd_ap, in_=rstd_ap, func=AF.Exp, scale=-0.5, bias=extra_bias)

    def norm_tile(xsrc, key_x, xn, ssq, hT_dst, tt, gs_i, sh_i, pbank):
        op("act", "activation", reads=[key_x], writes=["xn", "ssq0"],
           out=xn[:], in_=xsrc[:], func=AF.Square, accum_out=ssq[:, 0:1])
        rstd_ops(ssq[:, 0:1], ssq[:, 1:2], "ssq0", "ssq1", 1.0 / D)
        op("dve", "tensor_scalar", reads=[key_x, "ssq1"], writes=["xn"],
           out=xn[:], in0=xsrc[:], scalar1=ssq[:, 1:2], scalar2=None, op0=ALU.mult)
        for j in range(8):
            op("pe", "transpose", reads=["xn", "ident"], writes=bk(pbank + j // 4),
               out=ps_all[:, pbank * 512 + j * 128:pbank * 512 + (j + 1) * 128],
               in_=xn[:, j * 128:(j + 1) * 128], identity=ident[:])
        for j in range(8):
            op("dve", "tensor_scalar", reads=bk(pbank + j // 4) + [("mcols", gs_i), ("mcols", sh_i)], writes=["hT"],
               out=hT_dst[:, j, tt * 128:(tt + 1) * 128],
               in0=ps_all[:, pbank * 512 + j * 128:pbank * 512 + (j + 1) * 128],
               scalar1=mcols[:, gs_i, j:j + 1], scalar2=mcols[:, sh_i, j:j + 1], op0=ALU.mult, op1=ALU.add)

    ms = slopes()
    for l in range(DEPTH):
        lambda_init = 0.8 - 0.6 * math.exp(-0.3 * l)
        last = (l == DEPTH - 1)
        x_src = x_in if l == 0 else out
        xkeys = (lambda t: []) if l == 0 else (lambda t: [("xo", t)])

        P.barrier()
        A = Alloc(nc, BASE)
        modbc = A.t("modbc", [128, 6 * D], F32)
        wst = [A.t("wst%d" % i, [128, 8, 512], F32) for i in range(3)]
        dtmp = A.t("dtmp", [128, 4, 8, 128], F32)
        cab = A.t("cab", [128, 8, 128], F32)
        lamv = A.t("lamv", [128, 128], F32)
        lamp = A.t("lamp", [128, 64], F32)
        for j in range(8):
            op("dve", "tensor_scalar", reads=["ca", "ident"], writes=["cab"], out=cab[:, j, :], in0=ident[:, :],
               scalar1=0.0, scalar2=ca[:, j:j + 1], op0=ALU.mult, op1=ALU.add)
        ld(modbc[:], b_ada[l:l + 1, :].partition_broadcast(128), "m0", ["modbc"])
        wada_v = w_ada[l].rearrange("(j p) n -> p j n", p=128)
        for n in range(12):
            sl = n % 3
            ld(wst[sl][:], wada_v[:, :, n * 512:(n + 1) * 512], ("wst", sl), [("wst", sl)])
            for j in range(8):
                op("pe", "matmul", reads=["cab", ("wst", sl)], writes=bk(n % 2),
                   out=bank(n % 2), lhsT=cab[:, j, :], rhs=wst[sl][:, j, :], start=(j == 0), stop=(j == 7))
            op("dve", "tensor_tensor", reads=bk(n % 2) + ["modbc"], writes=["modbc"],
               out=modbc[:, n * 512:(n + 1) * 512], in0=modbc[:, n * 512:(n + 1) * 512], in1=bank(n % 2), op=ALU.add)
        op("dve", "tensor_copy", reads=["modbc"], writes=["GT0"], out=GT[:, 0, :], in_=modbc[:, 2 * D:3 * D])
        op("dve", "tensor_copy", reads=["modbc"], writes=["GT1"], out=GT[:, 1, :], in_=modbc[:, 5 * D:6 * D])
        for k, off in enumerate((0, D, 3 * D, 4 * D)):
            op("dve", "tensor_tensor", reads=["modbc", "ident"], writes=["dtmp"],
               out=dtmp[:, k, :, :], in0=modbc[:, off:off + D].rearrange("p (j m) -> p j m", m=128),
               in1=ident[:, :].unsqueeze(1).to_broadcast([128, 8, 128]), op=ALU.mult)
        for k, dst in ((0, 1), (1, 0), (2, 3), (3, 2)):
            op("dve", "tensor_reduce", reads=["dtmp"], writes=[("mcols", dst)],
               out=mcols[:, dst, :], in_=dtmp[:, k, :, :], axis=AX.X, op=ALU.add)
        for dst, goff in ((0, 0), (2, 8)):
            op("dve", "scalar_tensor_tensor", reads=[("mcols", dst), "gcol"], writes=[("mcols", dst)],
               out=mcols[:, dst, :], in0=mcols[:, dst, :], scalar=1.0,
               in1=gcol_t[:, l * 16 + goff:l * 16 + goff + 8], op0=ALU.add, op1=ALU.mult)
        ld(lamv[:], lam_in[l:l + 1, :].partition_broadcast(128), "m1", ["lamv"])
        lv = lamv[:].rearrange("p (a t b) -> p a t b", a=2, t=2)
        op("dve", "tensor_tensor", reads=["lamv"], writes=["lamp"], out=lamp[:].rearrange("p (a b) -> p a b", a=2),
           in0=lv[:, :, 0, :], in1=lv[:, :, 1, :], op=ALU.mult)
        op("dve", "tensor_reduce", reads=["lamp"], writes=["lamt23"], out=lamt[:, 2:4],
           in_=lamp[:].rearrange("p (a b) -> p a b", a=2), axis=AX.X, op=ALU.add)
        op("act", "activation", reads=["lamt23"], writes=["lamt45"], out=lamt[:, 4:6], in_=lamt[:, 2:4], func=AF.Exp)
        op("dve", "tensor_tensor", reads=["lamt45"], writes=["lamt0"], out=lamt[:, 0:1], in0=lamt[:, 4:5],
           in1=lamt[:, 5:6], op=ALU.subtract)
        op("dve", "tensor_scalar", reads=["lamt0"], writes=["lamt1"], out=lamt[:, 1:2], in0=lamt[:, 0:1],
           scalar1=lambda_init, scalar2=-1.0, op0=ALU.add, op1=ALU.mult)
        op("pool", "memset", writes=["cst1"], ap=cst[:, 1:2], constant=math.log(1.0 - lambda_init))

        P.barrier()
        A = Alloc(nc, BASE)
        QT = A.t("QT", [128, 4, S], BF16)
        KT = A.t("KT", [128, 4, S], BF16)
        V = A.t("V", [128, NT, NH, 128], BF16)
        A_att = A.off
        Wb = A.t("Wb", [128, 8, INW], BF16)
        A_tok = A.off
        stg = [A.t("stg%d" % i, [128, INW], F32) for i in range(2)]
        A = Alloc(nc, A_tok)
        xt = [A.t("xt%d" % i, [128, D], F32) for i in range(2)]
        xn = A.t("xn", [128, D], F32)
        hT = A.t("hT", [128, 8, 512], BF16)
        pst = [A.t("pst%d" % i, [128, 512], F32) for i in range(3)]
        ssq = A.t("ssq", [128, 4], F32)

        win_v = w_in[l].rearrange("(j p) n -> p j n", p=128)
        for j in range(8):
            sl = j % 2
            ld(stg[sl][:], win_v[:, j, :], ("stg", sl), [("stg", sl)])
            op("dve" if j % 2 == 0 else "pool", "tensor_copy", reads=[("stg", sl)], writes=[("Wb", j)],
               out=Wb[:, j, :], in_=stg[sl][:])
        op("pool", "memset", writes=["Vones"], ap=V[:, :, :, 64:128], constant=1.0)
        P.barrier()

        cnt = 0
        for i in range(NS):
            for tt in range(4):
                t = i * 4 + tt
                slot = t % 2
                ld(xt[slot][:], x_src[t * 128:(t + 1) * 128, :], ("xt", slot), [("xt", slot)], rkeys=xkeys(t))
                norm_tile(xt[slot], ("xt", slot), xn, ssq, hT, tt, 0, 1, (t % 2) * 2)
            for m in range(16):
                colbase = m * 128 if m < 8 else 1536 + (m - 8) * 128
                b = 4 + (cnt % 4)
                cnt += 1
                for j in range(8):
                    op("pe", "matmul", reads=[("Wb", j), "hT"], writes=bk(b), out=bank(b),
                       lhsT=Wb[:, j, colbase:colbase + 128], rhs=hT[:, j, :], start=(j == 0), stop=(j == 7))
                if m < 4:
                    op("act", "activation", reads=bk(b), writes=[("QT", i)], out=QT[:, m, i * 512:(i + 1) * 512],
                       in_=bank(b), func=AF.Copy, scale=32.0 ** -0.5)
                elif m < 8:
                    op("dve", "tensor_copy", reads=bk(b), writes=[("KT", i)], out=KT[:, m - 4, i * 512:(i + 1) * 512],
                       in_=bank(b))
                else:
                    s3 = m % 3
                    if m % 2 == 0:
                        op("act", "activation", reads=bk(b), writes=[("pst", s3)], out=pst[s3][:], in_=bank(b), func=AF.Copy)
                    else:
                        op("dve", "tensor_copy", reads=bk(b), writes=[("pst", s3)], out=pst[s3][:], in_=bank(b))
                    st(pc[m - 8, :, 8 + i * 512:8 + (i + 1) * 512], pst[s3][:], ("pst", s3), [("pst", s3)], [("pc", m - 8)])
            for tt in range(4):
                t = i * 4 + tt
                b = 4 + (cnt % 4)
                cnt += 1
                for j in range(8):
                    op("pe", "matmul", reads=[("Wb", j), "hT"], writes=bk(b), out=bank(b),
                       lhsT=hT[:, j, tt * 128:(tt + 1) * 128], rhs=Wb[:, j, 1024:1536], start=(j == 0), stop=(j == 7))
                if tt % 2 == 0:
                    op("act", "activation", reads=bk(b), writes=[("V", t)], out=V[:, t, :, 0:64],
                       in_=bank(b).rearrange("p (h d) -> p h d", d=64), func=AF.Copy)
                else:
                    op("dve", "tensor_copy", reads=bk(b), writes=[("V", t)], out=V[:, t, :, 0:64],
                       in_=bank(b).rearrange("p (h d) -> p h d", d=64))

        P.barrier()
        A = Alloc(nc, A_att)
        qrel = A.t("qrel", [128, 512], BF16)
        alL = A.t("alL", [128, NH, 128], BF16)
        alR = A.t("alR", [128, NH, 128], BF16)
        btab = A.t("btab", [128, NH, NB], F32)
        dtab = A.t("dtab", [128, 4, 512], F32)
        PT = [A.t("PT%d" % i, [128, 1024], BF16) for i in range(2)]
        Osb = A.t("Osb", [128, 1024], F32)
        Tm = A.t("Tm", [64, 1024], F32)
        att = A.t("att", [64, 512], F32)
        sqb = A.t("sqb", [64, 512], F32)
        rsb = A.t("rsb", [64, 512], F32)
        ya = [A.t("ya%d" % i, [64, 512], BF16) for i in range(2)]
        qrt = A.t("qrt", [128, 512], F32)
        NG = 2
        wst4 = [A.t("wst4_%d" % i, [128, DFF], F32) for i in range(NG)]
        wbo = [A.t("wbo%d" % i, [128, DFF], BF16) for i in range(2)]
        ld(qrt[:], c_qr, "c11", ["qrt"])
        ld(qrel[:], c_qrel, "c1", ["qrel"])
        ld(alL[:], c_alL.rearrange("p (h k) -> p h k", h=NH), "c2", ["alL"])
        ld(alR[:], c_alR.rearrange("p (h k) -> p h k", h=NH), "c3", ["alR"])
        ld(btab[:], c_bt.rearrange("p (h k) -> p h k", h=NH), "c4", ["btab"])
        ld(dtab[:], c_dt.rearrange("p (o q) -> p o q", o=4), "c5", ["dtab"])
        def keep(h, i, j):
            dmin = max(0, j * 128 - (i * 512 + 511), i * 512 - (j * 128 + 127))
            return ms[h] * dmin < SKIP_BIAS
        groups = [(h, i, j) for h in range(NH) for i in range(NS) for j in range(NT) if keep(h, i, j)]
        first_j = {}
        last_j = {}
        for (h_, i_, j_) in groups:
            first_j.setdefault((h_, i_), j_)
            last_j[(h_, i_)] = j_
        pending = []

        def emit_scores(n):
            h, i, j = groups[n]
            cch, rb, o = h // 2, (h % 2) * 64, j - 4 * i
            SB = 2 * (n % 2)
            diag = 0 <= o <= 3
            dve_bias = diag or (n % DVE_BIAS_MOD == 0)
            for p in range(2):
                r0 = rb + 32 * p
                op("pe", "matmul", reads=[("KT", j // 4), ("QT", i)], writes=bk(SB + p), out=bank(SB + p),
                   lhsT=KT[r0:r0 + 32, cch, j * 128:(j + 1) * 128], rhs=QT[r0:r0 + 32, cch, i * 512:(i + 1) * 512],
                   start=True, stop=dve_bias, tile_position=(r0, 0))
            if dve_bias and not diag:
                sgn = -ms[h] if o < 0 else ms[h]
                for p in range(2):
                    op("dve", "scalar_tensor_tensor", reads=["qrt"] + bk(SB + p), writes=bk(SB + p), out=bank(SB + p),
                       in0=qrt[:], scalar=sgn, in1=bank(SB + p), op0=ALU.mult, op1=ALU.add)
            elif not diag:
                al = alL if o < 0 else alR
                for p in range(2):
                    r0 = rb + 32 * p
                    op("pe", "matmul", reads=["alL", "alR", "qrel"], writes=bk(SB + p), out=bank(SB + p),
                       lhsT=al[r0:r0 + 2, h, :], rhs=qrel[r0:r0 + 2, :], start=False, stop=True, tile_position=(r0, 0))
            else:
                for p in range(2):
                    op("dve", "scalar_tensor_tensor", reads=["dtab"] + bk(SB + p), writes=bk(SB + p), out=bank(SB + p),
                       in0=dtab[:, o, :], scalar=-ms[h], in1=bank(SB + p), op0=ALU.mult, op1=ALU.add)

        def emit_exp(n):
            h, i, j = groups[n]
            o = j - 4 * i
            sb2 = n % 2
            op("act", "activation", reads=bk(2 * sb2, 2) + ["btab"], writes=[("PT", sb2)], out=PT[sb2][:], in_=bank(2 * sb2, 2),
               func=AF.Exp, bias=btab[:, h, o + NT:o + NT + 1], scale=1.0)

        def emit_pv(n):
            h, i, j = groups[n]
            sb2 = n % 2
            for p in range(2):
                op("pe", "matmul", reads=[("V", j), "Vones", ("PT", sb2)], writes=bk(4 + p),
                   out=ps_all[:, (4 + p) * 512:(5 + p) * 512], lhsT=V[:, j, h, :],
                   rhs=PT[sb2][:, p * 512:(p + 1) * 512], start=(j == first_j[(h, i)]), stop=(j == last_j[(h, i)]))
            for d_ in range(N_DUMMY):
                op("pe", "matmul", reads=[("V", j), "Vones", ("PT", sb2)], writes=bk(7), out=bank(7), lhsT=V[:, j, h, :],
                   rhs=PT[sb2][:, (d_ % 2) * 512:(d_ % 2 + 1) * 512], start=True, stop=True)

        def epilogue_stages(h, i):
            ys = (h * NS + i) % 2

            def s0():
                op("dve", "tensor_copy", reads=bk(4, 2), writes=["Osb"], out=Osb[0:65, :], in_=ps_all[0:65, 2048:3072])
                op("dve", "reciprocal", reads=["Osb"], writes=["Osb"], out=Osb[64:65, :], in_=Osb[64:65, :])
                op("dve", "tensor_scalar", reads=["Osb", "lamt1"], writes=["Osb"], out=Osb[64:65, 512:1024],
                   in0=Osb[64:65, 512:1024], scalar1=lamt[64:65, 1:2], scalar2=None, op0=ALU.mult)

            def s1():
                op("pe", "matmul", reads=["Osb", "sel"], writes=bk(6), out=ps_all[0:64, 3072:3584],
                   lhsT=sel[0:65, 0:64], rhs=Osb[0:65, 0:512], start=True, stop=True)

            def s2():
                op("dve", "tensor_tensor", reads=["Osb"] + bk(6), writes=["Tm"], out=Tm[:, 0:512], in0=Osb[0:64, 0:512],
                   in1=ps_all[0:64, 3072:3584], op=ALU.mult)

            def s1b():
                op("pe", "matmul", reads=["Osb", "sel"], writes=bk(6), out=ps_all[0:64, 3072:3584],
                   lhsT=sel[0:65, 0:64], rhs=Osb[0:65, 512:1024], start=True, stop=True)

            def s2b():
                op("dve", "tensor_tensor", reads=["Osb"] + bk(6), writes=["Tm"], out=Tm[:, 512:1024], in0=Osb[0:64, 512:1024],
                   in1=ps_all[0:64, 3072:3584], op=ALU.mult)
                op("dve", "tensor_tensor", reads=["Tm"], writes=["att"], out=att[:], in0=Tm[:, 0:512], in1=Tm[:, 512:1024], op=ALU.add)

            def s3():
                op("act", "activation", reads=["att"], writes=["sqb"], out=sqb[:], in_=att[:], func=AF.Square)

            def s4():
                op("pe", "matmul", reads=["sqb", "ones32"], writes=bk(6), out=ps_all[0:64, 3072:3584], lhsT=ones32[0:64, 0:64],
                   rhs=sqb[:], start=True, stop=True)

            def s5():
                rstd_ops(ps_all[0:64, 3072:3584], rsb[:], ("ps", 6), "rsb", 1.0 / 64, npart=64, extra_bias=cst[0:64, 1:2])

            def s6():
                op("dve", "scalar_tensor_tensor", reads=["att", "rsb", "gsubraw"], writes=[("ya", ys)], out=ya[ys][:], in0=att[:],
                   scalar=gsub_t[:, l * 8 + h:l * 8 + h + 1], in1=rsb[:], op0=ALU.mult, op1=ALU.mult)
                st(yts[h * 64:(h + 1) * 64, i * 512:(i + 1) * 512], ya[ys][:], ("ya", ys), [("ya", ys)], [("yts", h // 2)])

            return [s0, s1, s2, s1b, s2b, s3, s4, s5, s6]

        wgu_v = w_gu[l].rearrange("(j p) n -> p j n", p=128)
        wd_v = w_down[l].rearrange("(f p) n -> p f n", p=128)
        wgub_v = wgu_bf.rearrange("(j p) n -> p j n", p=128)
        wdb_v = wd_bf.rearrange("(f p) n -> p f n", p=128)
        witems = [("gu", j_, half_) for j_ in range(8) for half_ in range(2)] + [("d", f_, 0) for f_ in range(0, 22, 2)]
        wstate = {"k": 0}

        def emit_witem(item):
            k_ = wstate["k"]
            wstate["k"] += 1
            sl, so = k_ % NG, k_ % 2
            if item[0] == "gu":
                _, j_, half_ = item
                ld(wst4[sl][:], wgu_v[:, j_, half_ * DFF:(half_ + 1) * DFF], ("wst4", sl), [("wst4", sl)])
                op("pool", "tensor_copy", reads=[("wst4", sl)], writes=[("wbo", so)], out=wbo[so][:], in_=wst4[sl][:])
                st(wgub_v[:, j_, half_ * DFF:(half_ + 1) * DFF], wbo[so][:], ("wbo", so), [("wbo", so)], [("wgubf", j_)])
            else:
                _, f_, _ = item
                ld(wst4[sl][:, 0:2 * D].rearrange("p (f n) -> p f n", f=2), wd_v[:, f_:f_ + 2, :], ("wst4", sl), [("wst4", sl)])
                op("pool", "tensor_tensor", reads=[("wst4", sl), "GT1"], writes=[("wbo", so)],
                   out=wbo[so][:, 0:2 * D].rearrange("p (f n) -> p f n", f=2),
                   in0=wst4[sl][:, 0:2 * D].rearrange("p (f n) -> p f n", f=2),
                   in1=GT[:, 1:2, :].to_broadcast([128, 2, D]), op=ALU.mult)
                st(wdb_v[:, f_:f_ + 2, :], wbo[so][:, 0:2 * D].rearrange("p (f n) -> p f n", f=2), ("wbo", so), [("wbo", so)],
                   [("wdbf", f_ // 2)])

        wgap = max(1, len(groups) // (len(witems) + 2))
        emit_scores(0)
        for n in range(len(groups)):
            h, i, j = groups[n]
            if witems and n % wgap == wgap - 1:
                emit_witem(witems.pop(0))
            emit_exp(n)
            if n + 1 < len(groups):
                emit_scores(n + 1)
            emit_pv(n)
            if j == last_j[(h, i)]:
                while pending:
                    pending.pop(0)()
                stg_list = epilogue_stages(h, i)
                stg_list[0]()
                pending = stg_list[1:]
            elif pending and (j % EPI_GAP == EPI_GAP - 1 or NT <= 8):
                pending.pop(0)()
        while pending:
            pending.pop(0)()
        while witems:
            emit_witem(witems.pop(0))

        P.barrier()
        A = Alloc(nc, BASE)
        fb = [A.t("fb%d" % i, [128, SP], F32) for i in range(4)]
        ob = A.t("ob", [128, S], BF16)
        wpbd = A.t("wpbd", [128, 128], F32)
        wpb = A.t("wpb", [128, 128], BF16)
        yb = [A.t("yb%d" % i, [128, 512], BF16) for i in range(2)]
        pco = l * 8
        FB = lambda k: ("fb", k)
        for ci in range(2):
            ld(fb[0][:], pc[4 + ci], ("fb", 0), [FB(0)], rkeys=[("pc", 4 + ci), ("pcpad", 4 + ci)])
            ld(fb[1][:], pc[6 + ci], ("fb", 1), [FB(1)], rkeys=[("pc", 6 + ci), ("pcpad", 6 + ci)])
            ld(fb[2][:], pc[2 + ci], ("fb", 2), [FB(2)], rkeys=[("pc", 2 + ci), ("pcpad", 2 + ci)])
            op("dve", "tensor_tensor", reads=[FB(0), FB(1)], writes=[FB(0)], out=fb[0][:], in0=fb[0][:], in1=fb[1][:], op=ALU.mult)
            wc = [pcol_t[:, pco + 2 + jj * 2 + ci:pco + 2 + jj * 2 + ci + 1] for jj in range(3)]
            op("dve", "tensor_scalar", reads=[FB(0), "pcol"], writes=[FB(1)], out=fb[1][:, 8:8 + S], in0=fb[0][:, 7:7 + S],
               scalar1=wc[0], scalar2=None, op0=ALU.mult)
            op("dve", "scalar_tensor_tensor", reads=[FB(0), FB(1), "pcol"], writes=[FB(3)], out=fb[3][:, 8:8 + S],
               in0=fb[0][:, 8:8 + S], scalar=wc[1], in1=fb[1][:, 8:8 + S], op0=ALU.mult, op1=ALU.add)
            op("dve", "scalar_tensor_tensor", reads=[FB(0), FB(3), "pcol"], writes=[FB(1)], out=fb[1][:, 8:8 + S],
               in0=fb[0][:, 9:9 + S], scalar=wc[2], in1=fb[3][:, 8:8 + S], op0=ALU.mult, op1=ALU.add)
            op("dve", "tensor_tensor", reads=[FB(1), FB(2)], writes=["ob"], out=ob[:], in0=fb[1][:, 8:8 + S],
               in1=fb[2][:, 8:8 + S], op=ALU.mult)
            st(yts[768 + ci * 128:768 + (ci + 1) * 128, :], ob[:], "ob", ["ob"], [("yts", 6 + ci)])
        for ci in range(2):
            ld(fb[0][:], pc[ci], ("fb", 0), [FB(0)], rkeys=[("pc", ci), ("pcpad", ci)])
            op("pool", "memset", writes=["wpbd"], ap=wpbd[:], constant=0.0)
            for g2 in range(2):
                ld(wpbd[g2 * 64:(g2 + 1) * 64, g2 * 64:(g2 + 1) * 64], w_pool[l, ci * 2 + g2], ("wp", g2), ["wpbd"])
            op("dve", "tensor_copy", reads=["wpbd"], writes=["wpb"], out=wpb[:], in_=wpbd[:])
            u, p2, p4, p8 = fb[0], fb[1], fb[2], fb[3]
            op("dve", "tensor_tensor", reads=[FB(0)], writes=[FB(1)], out=p2[:, 0:SP - 1], in0=u[:, 0:SP - 1], in1=u[:, 1:SP], op=ALU.add)
            if ci == 0:
                op("dve", "tensor_copy", reads=[FB(1)], writes=[FB(2)], out=fb[2][0:64, 8:8 + S], in_=p2[0:64, 7:7 + S])
                op("dve", "tensor_tensor", reads=[FB(1)], writes=[FB(2)], out=fb[2][64:128, 8:8 + S],
                   in0=p2[64:128, 6:6 + S], in1=p2[64:128, 8:8 + S], op=ALU.add)
                sb_, skey = fb[2], FB(2)
            else:
                op("dve", "tensor_tensor", reads=[FB(1)], writes=[FB(2)], out=p4[:, 0:SP - 3], in0=p2[:, 0:SP - 3],
                   in1=p2[:, 2:SP - 1], op=ALU.add)
                op("dve", "tensor_tensor", reads=[FB(2)], writes=[FB(3)], out=p8[64:128, 0:SP - 7], in0=p4[64:128, 0:SP - 7],
                   in1=p4[64:128, 4:SP - 3], op=ALU.add)
                op("dve", "tensor_tensor", reads=[FB(2), FB(1)], writes=[FB(1)], out=fb[1][0:64, 8:8 + S],
                   in0=p4[0:64, 4:4 + S], in1=p4[0:64, 8:8 + S], op=ALU.add)
                op("dve", "tensor_tensor", reads=[FB(3), FB(1)], writes=[FB(1)], out=fb[1][64:128, 8:8 + S],
                   in0=p8[64:128, 0:S], in1=p8[64:128, 8:8 + S], op=ALU.add)
                sb_, skey = fb[1], FB(1)
            op("dve", "tensor_scalar", reads=[skey, "poolc"], writes=[skey], out=sb_[:, 8:8 + S], in0=sb_[:, 8:8 + S],
               scalar1=poolc[:, ci, 0:1], scalar2=None, op0=ALU.mult)
            op("dve", "tensor_tensor", reads=[skey, "poolc"], writes=[skey], out=sb_[:, 8:16], in0=sb_[:, 8:16],
               in1=poolc[:, ci, 1:9], op=ALU.mult)
            op("dve", "tensor_tensor", reads=[skey, "poolc"], writes=[skey], out=sb_[:, S:S + 8], in0=sb_[:, S:S + 8],
               in1=poolc[:, ci, 9:17], op=ALU.mult)
            op("dve", "tensor_tensor", reads=[skey, FB(0)], writes=["ob"], out=ob[:], in0=sb_[:, 8:8 + S], in1=u[:, 8:8 + S],
               op=ALU.subtract)
            for i in range(NS):
                b = i % 4
                op("pe", "matmul", reads=["wpb", "ob"], writes=bk(b), out=bank(b), lhsT=wpb[:], rhs=ob[:, i * 512:(i + 1) * 512],
                   start=True, stop=True)
                ys = i % 2
                op("act", "activation", reads=bk(b) + ["pcol"], writes=[("yb", ys)], out=yb[ys][:], in_=bank(b),
                   func=AF.Identity, scale=pcol_t[:, pco + ci:pco + ci + 1])
                st(yts[512 + ci * 128:512 + (ci + 1) * 128, i * 512:(i + 1) * 512], yb[ys][:], ("yb", ys), [("yb", ys)],
                   [("yts", 4 + ci)])

        P.barrier()
        A = Alloc(nc, BASE)
        Wo = A.t("Wo", [128, 8, D], BF16)
        wos = [A.t("wos%d" % i, [128, D], F32) for i in range(2)]
        Yt = [A.t("Yt%d" % i, [128, 8, 512], BF16) for i in range(2)]
        xb = [A.t("xb%d" % i, [128, D], F32) for i in range(2)]
        wout_v = w_out[l].rearrange("(j p) n -> p j n", p=128)
        for j in range(8):
            sl = j % 2
            ld(wos[sl][:], wout_v[:, j, :], ("wos", sl), [("wos", sl)])
            op("dve", "tensor_tensor", reads=[("wos", sl), "GT0"], writes=[("Wo", j)], out=Wo[:, j, :], in0=wos[sl][:],
               in1=GT[:, 0, :], op=ALU.mult)
        yts_v = yts.rearrange("(c p) s -> p c s", p=128)
        for i in range(NS):
            ysl = i % 2
            ld(Yt[ysl][:], yts_v[:, :, i * 512:(i + 1) * 512], ("Yt", ysl), [("Yt", ysl)], rkeys=[("yts", c8) for c8 in range(8)])
            for tt in range(4):
                t = i * 4 + tt
                slot = t % 2
                ld(xb[slot][:], x_src[t * 128:(t + 1) * 128, :], ("xb", slot), [("xb", slot)], rkeys=xkeys(t))
                pb = (t % 2) * 2
                for half in range(2):
                    for c8 in range(8):
                        op("pe", "matmul", reads=[("Yt", ysl), ("Wo", c8)], writes=bk(pb + half), out=bank(pb + half),
                           lhsT=Yt[ysl][:, c8, tt * 128:(tt + 1) * 128], rhs=Wo[:, c8, half * 512:(half + 1) * 512],
                           start=(c8 == 0), stop=(c8 == 7))
                op("dve", "tensor_tensor", reads=[("xb", slot)] + bk(pb, 2), writes=[("xb", slot)], out=xb[slot][:],
                   in0=xb[slot][:], in1=bank(pb, 2), op=ALU.add)
                st(out[t * 128:(t + 1) * 128, :], xb[slot][:], ("xbs", slot), [("xb", slot)], [("xo", t)])

        P.barrier()
        A = Alloc(nc, BASE)
        Wgu = A.t("Wgu", [128, 8, 2 * DFF], BF16)
        Wd = A.t("Wd", [128, 22, D], BF16)
        X1 = [A.t("X1_%d" % i, [128, D], F32) for i in range(4)]
        xn2 = A.t("xn2", [128, D], F32)
        hT2 = A.t("hT2", [128, 8, 512], BF16)
        sg = [A.t("sg%d" % i, [128, 512], F32) for i in range(2)]
        ssq2 = A.t("ssq2", [128, 4], F32)
        A_h = A.off
        Hh = A.t("Hh", [128, 22, 512], BF16)
        for j in range(8):
            ld(Wgu[:, j, :], wgub_v[:, j, :], ("wgl", j % 4), [("Wgu", j)], rkeys=[("wgubf", j)])
        for f in range(0, 22, 2):
            ld(Wd[:, f:f + 2, :], wdb_v[:, f:f + 2, :], ("wdl", (f // 2) % 4), [("Wd", f), ("Wd", f + 1)], rkeys=[("wdbf", f // 2)])
        P.barrier()
        gfb = xn2
        if last:
            A = Alloc(nc, A_h + 22 * 512 * 2)
            gfb = A.t("gfb", [128, D], F32)
            ld(gfb[:], gfin[0:1, :].partition_broadcast(128), "gf", ["gfb"])
        gi = 0
        for i in range(NS):
            for tt in range(4):
                t = i * 4 + tt
                ld(X1[tt][:], out[t * 128:(t + 1) * 128, :], ("X1", tt), [("X1", tt)], rkeys=[("xo", t)])
                norm_tile(X1[tt], ("X1", tt), xn2, ssq2, hT2, tt, 2, 3, 0)
            for f in range(22):
                gb = 2 + 2 * (gi % 2)
                sgs = gi % 2
                gi += 1
                for which in range(2):
                    cb = which * DFF + f * 128
                    for j in range(8):
                        op("pe", "matmul", reads=[("Wgu", j), "hT"], writes=bk(gb + which), out=bank(gb + which),
                           lhsT=Wgu[:, j, cb:cb + 128], rhs=hT2[:, j, :], start=(j == 0), stop=(j == 7))
                op("act", "activation", reads=bk(gb), writes=[("sg", sgs)], out=sg[sgs][:], in_=bank(gb), func=AF.Silu)
                op("dve", "tensor_tensor", reads=[("sg", sgs)] + bk(gb + 1), writes=["Hh"], out=Hh[:, f, :], in0=sg[sgs][:],
                   in1=bank(gb + 1), op=ALU.mult)
            for tt in range(4):
                t = i * 4 + tt
                for half in range(2):
                    for f in range(22):
                        op("pe", "matmul", reads=["Hh", ("Wd", f)], writes=bk(6 + half), out=bank(6 + half),
                           lhsT=Hh[:, f, tt * 128:(tt + 1) * 128], rhs=Wd[:, f, half * 512:(half + 1) * 512],
                           start=(f == 0), stop=(f == 21))
                op("dve", "tensor_tensor", reads=[("X1", tt)] + bk(6, 2), writes=[("X1", tt)], out=X1[tt][:], in0=X1[tt][:],
                   in1=bank(6, 2), op=ALU.add)
                if last:
                    op("act", "activation", reads=[("X1", tt)], writes=["xn", "ssq2"], out=xn2[:], in_=X1[tt][:],
                       func=AF.Square, accum_out=ssq2[:, 2:3])
                    rstd_ops(ssq2[:, 2:3], ssq2[:, 3:4], "ssq2", "ssq3", 1.0 / D)
                    op("dve", "scalar_tensor_tensor", reads=[("X1", tt), "ssq3", "gfb"], writes=[("X1", tt)], out=X1[tt][:],
                       in0=X1[tt][:], scalar=ssq2[:, 3:4], in1=gfb[:], op0=ALU.mult, op1=ALU.mult)
                st(out[t * 128:(t + 1) * 128, :], X1[tt][:], ("x1s", tt), [("X1", tt)], [("xo", t)])
    P.emit()
    return nc, P


def make_consts(S):
    NT = S // 128
    NB = 2 * NT
    bf = ml_dtypes.bfloat16
    ms = slopes()
    ident = np.eye(128, dtype=np.float32)
    q = np.arange(512)
    qrel = np.zeros((128, 512), np.float32)
    for r in range(4):
        qrel[32 * r + 0] = q % 256
        qrel[32 * r + 1] = 256 * (q >= 256)
    alL = np.zeros((128, NH, 128), np.float32)
    alR = np.zeros((128, NH, 128), np.float32)
    for h in range(NH):
        rb = (h % 2) * 64
        for p in range(2):
            for rr in range(2):
                alL[rb + 32 * p + rr, h, :] = -ms[h]
                alR[rb + 32 * p + rr, h, :] = ms[h]
    krel = np.arange(128, dtype=np.float64)
    bt = np.zeros((128, NH, NB), np.float64)
    for h in range(NH):
        for o in range(-NT + 1, NT):
            if o < 0:
                bt[:, h, o + NT] = ms[h] * krel + 128.0 * ms[h] * o
            elif o > 3:
                bt[:, h, o + NT] = -ms[h] * krel - 128.0 * ms[h] * o
    dtb = np.zeros((128, 4, 512), np.float32)
    for o in range(4):
        dtb[:, o, :] = np.abs(q[None, :] - krel[:, None] - 128 * o)
    poolc = np.zeros((128, 2, 17), np.float32)
    wins = (2, 4, 8, 16)
    for ci in range(2):
        for g2 in range(2):
            w = wins[ci * 2 + g2]
            rows = slice(g2 * 64, (g2 + 1) * 64)
            poolc[rows, ci, 0] = 1.0 / w
            for e_ in range(8):
                for side, t in ((0, e_), (1, S - 8 + e_)):
                    lo = min(max(t - w // 2, 0), S)
                    hi = min(max(t + w - w // 2, 0), S)
                    poolc[rows, ci, 1 + side * 8 + e_] = float(w) / float(hi - lo)
    return {
        "c_ident": ident,
        "c_qrel": qrel.astype(bf),
        "c_alL": alL.reshape(128, NH * 128).astype(bf),
        "c_alR": alR.reshape(128, NH * 128).astype(bf),
        "c_bt": bt.reshape(128, NH * NB).astype(np.float32),
        "c_dt": dtb.reshape(128, 4 * 512),
        "c_pool": poolc.reshape(128, 34),
        "c_qr": np.ascontiguousarray(np.broadcast_to(q.astype(np.float32)[None, :], (128, 512))),
    }


def col128(v):
    v = np.asarray(v, np.float32)
    return np.ascontiguousarray(v.reshape(-1, 128).T)


def make_in_maps(S, x, c, w_ada, b_ada, g_mix, w_in, lambda_q1, lambda_k1, lambda_q2, lambda_k2,
                 g_subln, w_pool, pool_scale, conv_w, w_out, g_ffn, w_gate_up, w_down, g_final):
    f = lambda a: np.ascontiguousarray(np.asarray(a, np.float32))
    B = x.shape[0]
    consts = make_consts(S)
    gcols = np.concatenate([np.concatenate([col128(g_mix[l]), col128(g_ffn[l])], axis=1) for l in range(DEPTH)], axis=1)
    lam = np.stack([np.concatenate([f(lambda_q1[l]), f(lambda_k1[l]), f(lambda_q2[l]), f(lambda_k2[l])]) for l in range(DEPTH)])
    gsub = np.concatenate([f(g_subln[l]).reshape(NH, 64).T for l in range(DEPTH)], axis=1)
    pcs = []
    for l in range(DEPTH):
        cols = [col128(pool_scale[l])]
        cw = f(conv_w[l])
        for jj in range(3):
            cols.append(col128(cw[jj]))
        pcs.append(np.concatenate(cols, axis=1))
    pcols = np.concatenate(pcs, axis=1)
    shared = {
        "w_ada": f(w_ada), "b_ada": f(b_ada), "gcols": np.ascontiguousarray(gcols), "w_in": f(w_in),
        "lam": np.ascontiguousarray(lam), "gsub": np.ascontiguousarray(gsub), "w_pool": f(w_pool),
        "pcols": np.ascontiguousarray(pcols), "w_out": f(w_out), "w_gate_up": f(w_gate_up), "w_down": f(w_down),
        "gfin": f(g_final).reshape(1, D),
    }
    shared.update(consts)
    maps = []
    xf = f(x)
    cf = f(c)
    for b in range(B):
        m = dict(shared)
        m["x"] = np.ascontiguousarray(xf[b])
        m["cT"] = col128(cf[b])
        maps.append(m)
    return maps


_CACHE = {}


def kernel(**inputs):
    x = np.asarray(inputs["x"])
    B, S, _ = x.shape
    if S not in _CACHE:
        _CACHE[S] = build(S)[0]
    nc = _CACHE[S]
    in_maps = make_in_maps(S, **inputs)
    res = run_bass_kernel_spmd(nc, in_maps, core_ids=list(range(B)))
    return np.stack([np.asarray(r["out"], np.float32) for r in res.results], axis=0)
```

```python
import math
import numpy as np
import ml_dtypes
import concourse.bass as bass
import concourse.mybir as mybir
from concourse.bass_utils import run_bass_kernel_spmd
from contextlib import ExitStack

F32 = mybir.dt.float32
BF16 = mybir.dt.bfloat16
AF = mybir.ActivationFunctionType
ALU = mybir.AluOpType
AX = mybir.AxisListType

D = 1024
NH = 8
DFF = 2816
INW = 2560
DEPTH = 2
EPS = 1e-6
EPOCH = 12000
EPI_GAP = 2
SKIP_BIAS = 150.0
DVE_BIAS_MOD = 10 ** 9
SBUF_BASE = 16512
SBUF_LIMIT = 229376


class Op:
    __slots__ = ("eng", "fn", "deps", "is_dma", "semkey", "sig", "val", "semid")

    def __init__(self, eng, fn, is_dma, semkey):
        self.eng = eng
        self.fn = fn
        self.deps = []
        self.is_dma = is_dma
        self.semkey = semkey
        self.sig = False
        self.val = None
        self.semid = None


class Prog:
    ENGS = ("pe", "act", "dve", "pool", "sp")

    def __init__(self, nc):
        self.nc = nc
        self.ops = {e: [] for e in self.ENGS}
        self.last_writer = {}
        self.readers = {}
        self.last_compute = {}
        self.last_dma = {}
        self.bar_deps = []
        self.bar_gen = 0
        self.eng_gen = {e: 0 for e in self.ENGS}

    def add(self, eng, method, reads=(), writes=(), dma=False, semkey=None, **kw):
        def fn(e, method=method, kw=kw):
            return getattr(e, method)(**kw)
        op = Op(eng, fn, dma, semkey if dma else None)
        if dma and semkey is None:
            op.semkey = ("dmaq", eng)
        deps = {}
        for b in reads:
            w = self.last_writer.get(b)
            if w is not None:
                deps[id(w)] = (w, "RAW")
        for b in writes:
            w = self.last_writer.get(b)
            if w is not None and id(w) not in deps:
                deps[id(w)] = (w, "WAW")
            for r in self.readers.get(b, ()):
                if id(r) not in deps:
                    deps[id(r)] = (r, "WAR")
        if self.eng_gen[eng] < self.bar_gen:
            self.eng_gen[eng] = self.bar_gen
            for d in self.bar_deps:
                deps[id(d)] = (d, "BAR")
        for d, kind in deps.values():
            if d is op:
                continue
            same = (d.eng == eng) and (not d.is_dma) and (not dma)
            if same:
                if eng == "pe" or kind in ("WAR", "BAR"):
                    continue
            op.deps.append(d)
            d.sig = True
        for b in reads:
            self.readers.setdefault(b, []).append(op)
        for b in writes:
            self.last_writer[b] = op
            self.readers[b] = []
        self.ops[eng].append(op)
        if dma:
            self.last_dma[op.semkey] = op
        else:
            self.last_compute[eng] = op
        return op

    def barrier(self):
        self.bar_deps = list(self.last_compute.values()) + list(self.last_dma.values())
        self.bar_gen += 1

    def emit(self):
        nc = self.nc
        finals = list(self.last_compute.values()) + list(self.last_dma.values())
        for d in finals:
            d.sig = True
        counters = {}
        semids = {}
        for e in self.ENGS:
            for op in self.ops[e]:
                if not op.sig:
                    continue
                key = op.semkey if op.is_dma else ("eng", e)
                c = counters.get(key, 0)
                counters[key] = c + 1
                op.semid = (key, c // EPOCH)
                op.val = (c % EPOCH + 1) * (16 if op.is_dma else 1)
                semids[op.semid] = None
        with ExitStack() as st:
            sems = {}
            for i, k in enumerate(semids):
                sems[k] = st.enter_context(nc.semaphore("s%d" % i))
            self.n_sems = len(sems)
            block = st.enter_context(nc.Block())
            engmap = {"pe": block.tensor, "act": block.scalar, "dve": block.vector,
                      "pool": block.gpsimd, "sp": block.sync}

            def make(e):
                def body(eng):
                    waited = {}

                    def wait(d):
                        if waited.get(d.semid, 0) >= d.val:
                            return
                        eng.wait_ge(sems[d.semid], d.val)
                        waited[d.semid] = d.val

                    for op in self.ops[e]:
                        for d in op.deps:
                            wait(d)
                        ins = op.fn(eng)
                        if op.sig:
                            ins.then_inc(sems[op.semid], 16 if op.is_dma else 1)
                    if e == "sp":
                        for d in finals:
                            if d.eng == "sp" and not d.is_dma:
                                continue
                            wait(d)
                return body

            for e in self.ENGS:
                engmap[e](make(e))


class Alloc:
    def __init__(self, nc, base, limit=SBUF_LIMIT):
        self.nc = nc
        self.off = base
        self.limit = limit
        self.n = 0

    def t(self, name, shape, dtype):
        nb = int(np.prod(shape[1:])) * mybir.dt.size(dtype)
        nb = (nb + 31) // 32 * 32
        self.n += 1
        h = self.nc.alloc_sbuf_tensor_at("%s_%d_%d" % (name, self.off, self.n), list(shape), dtype, offset=self.off)
        self.off += nb
        assert self.off <= self.limit, ("SBUF overflow", name, self.off)
        return h


def slopes():
    return [2.0 ** (-(h + 1)) for h in range(NH)]


def build(S):
    NT = S // 128
    NS = S // 512
    NB = 2 * NT
    SP = S + 16
    nc = bass.Bass("TRN2", target_bir_lowering=False)

    def din(name, shape, dtype=F32):
        return nc.dram_tensor(name, list(shape), dtype, kind="ExternalInput").ap()

    x_in = din("x", [S, D])
    cT = din("cT", [128, 8])
    w_ada = din("w_ada", [DEPTH, D, 6 * D])
    b_ada = din("b_ada", [DEPTH, 6 * D])
    gcols = din("gcols", [128, DEPTH * 16])
    w_in = din("w_in", [DEPTH, D, INW])
    lam_in = din("lam", [DEPTH, 128])
    gsub = din("gsub", [64, DEPTH * 8])
    w_pool = din("w_pool", [DEPTH, 4, 64, 64])
    pcols = din("pcols", [128, DEPTH * 8])
    w_out = din("w_out", [DEPTH, D, D])
    w_gu = din("w_gate_up", [DEPTH, D, 2 * DFF])
    w_down = din("w_down", [DEPTH, DFF, D])
    gfin = din("gfin", [1, D])
    c_ident = din("c_ident", [128, 128])
    c_qrel = din("c_qrel", [128, 512], BF16)
    c_alL = din("c_alL", [128, NH * 128], BF16)
    c_alR = din("c_alR", [128, NH * 128], BF16)
    c_bt = din("c_bt", [128, NH * NB])
    c_dt = din("c_dt", [128, 4 * 512])
    c_pool = din("c_pool", [128, 2 * 17])
    c_qr = din("c_qr", [128, 512])
    out = nc.dram_tensor("out", [S, D], F32, kind="ExternalOutput").ap()
    pc = nc.dram_tensor("pc_scr", [8, 128, SP], F32, kind="Internal").ap()
    yts = nc.dram_tensor("yt_scr", [D, S], BF16, kind="Internal").ap()
    wgu_bf = nc.dram_tensor("wgu_bf", [D, 2 * DFF], BF16, kind="Internal").ap()
    wd_bf = nc.dram_tensor("wd_bf", [DFF, D], BF16, kind="Internal").ap()

    P = Prog(nc)
    op = P.add
    ps_all = nc.alloc_psum_tensor("ps_all", [128, 4096], F32)

    def bank(b, n=1):
        return ps_all[:, b * 512:(b + n) * 512]

    def bk(b, n=1):
        return [("ps", b + i) for i in range(n)]

    def ld(dst, src, key, wkeys, rkeys=(), q="sp"):
        return op(q, "dma_start", reads=list(rkeys), writes=list(wkeys), dma=True, semkey=key, out=dst, in_=src)

    def st(dst, src, key, rkeys, wkeys, q="pool"):
        return op(q, "dma_start", reads=list(rkeys), writes=list(wkeys), dma=True, semkey=key, out=dst, in_=src)

    A0 = Alloc(nc, SBUF_BASE)
    ident = A0.t("ident", [128, 128], F32)
    ones32 = A0.t("ones32", [128, 64], F32)
    sel = A0.t("sel", [128, 64], F32)
    poolc = A0.t("poolc", [128, 2, 17], F32)
    gcol_t = A0.t("gcol", [128, DEPTH * 16], F32)
    pcol_t = A0.t("pcol", [128, DEPTH * 8], F32)
    gsub_t = A0.t("gsub", [64, DEPTH * 8], F32)
    cst = A0.t("cst", [128, 8], F32)
    zpad = A0.t("zpad", [128, 8], F32)
    ca = A0.t("ca", [128, 8], F32)
    GT = A0.t("GT", [128, 2, D], F32)
    mcols = A0.t("mcols", [128, 4, 8], F32)
    lamt = A0.t("lamt", [128, 8], F32)
    BASE = A0.off

    ld(ident[:], c_ident, "c0", ["ident"])
    ld(poolc[:], c_pool.rearrange("p (c k) -> p c k", c=2), "c6", ["poolc"])
    ld(gcol_t[:], gcols, "c7", ["gcol"])
    ld(pcol_t[:], pcols, "c8", ["pcol"])
    ld(gsub_t[:], gsub, "c9", ["gsubraw"])
    ld(ca[:], cT, "c10", ["ca"])
    op("pool", "memset", writes=["ones32"], ap=ones32[:], constant=1.0)
    op("pool", "memset", writes=["sel"], ap=sel[:], constant=0.0)
    op("pool", "memset", writes=["sel"], ap=sel[64:65, :], constant=1.0)
    op("pool", "memset", writes=["zpad"], ap=zpad[:], constant=0.0)
    op("pool", "memset", writes=["cst0"], ap=cst[:, 0:1], constant=EPS)
    op("pool", "memset", writes=["cst2"], ap=cst[:, 2:3], constant=0.0)
    for ch in range(8):
        st(pc[ch, :, 0:8], zpad[:], ("zp", 0), ["zpad"], [("pcpad", ch)])
        st(pc[ch, :, S + 8:S + 16], zpad[:], ("zp", 1), ["zpad"], [("pcpad", ch)])
    op("act", "activation", reads=["ca"], writes=["ca"], out=ca[:], in_=ca[:], func=AF.Silu)

    def rstd_ops(ss_ap, rstd_ap, key_ss, key_rstd, scale, npart=128, extra_bias=None):
        op("act", "activation", reads=[key_ss, "cst0"], writes=[key_rstd],
           out=rstd_ap, in_=ss_ap, func=AF.Ln, scale=scale, bias=cst[0:npart, 0:1])
        if extra_bias is None:
            op("act", "activation", reads=[key_rstd], writes=[key_rstd], out=rstd_ap, in_=rstd_ap, func=AF.Exp, scale=-0.5)
        else:
            op("act", "activation", reads=[key_rstd, "cst1"], writes=[key_rstd],
               out=rstd_ap, in_=rstd_ap, func=AF.Exp, scale=-0.5, bias=extra_bias)

    def norm_p1(xsrc, key_x, xn, ssq):
        op("act", "activation", reads=[key_x], writes=["xn", "ssq0"],
           out=xn[:], in_=xsrc[:], func=AF.Square, accum_out=ssq[:, 0:1])
        rstd_ops(ssq[:, 0:1], ssq[:, 1:2], "ssq0", "ssq1", 1.0 / D)
        op("dve", "tensor_scalar", reads=[key_x, "ssq1"], writes=["xn"],
           out=xn[:], in0=xsrc[:], scalar1=ssq[:, 1:2], scalar2=None, op0=ALU.mult)

    def norm_p2(xn, hT_dst, tt, gs_i, sh_i, pbank, hkey="hT"):
        for j in range(8):
            op("pe", "transpose", reads=["xn", "ident"], writes=bk(pbank + j // 4),
               out=ps_all[:, pbank * 512 + j * 128:pbank * 512 + (j + 1) * 128],
               in_=xn[:, j * 128:(j + 1) * 128], identity=ident[:])
        for j in range(8):
            op("dve", "tensor_scalar", reads=bk(pbank + j // 4) + [("mcols", gs_i), ("mcols", sh_i)], writes=[hkey],
               out=hT_dst[:, j, tt * 128:(tt + 1) * 128],
               in0=ps_all[:, pbank * 512 + j * 128:pbank * 512 + (j + 1) * 128],
               scalar1=mcols[:, gs_i, j:j + 1], scalar2=mcols[:, sh_i, j:j + 1], op0=ALU.mult, op1=ALU.add)

    def norm_tile(xsrc, key_x, xn, ssq, hT_dst, tt, gs_i, sh_i, pbank, hkey="hT"):
        norm_p1(xsrc, key_x, xn, ssq)
        norm_p2(xn, hT_dst, tt, gs_i, sh_i, pbank, hkey)

    ms = slopes()
    for l in range(DEPTH):
        lambda_init = 0.8 - 0.6 * math.exp(-0.3 * l)
        last = (l == DEPTH - 1)
        x_src = x_in if l == 0 else out
        xkeys = (lambda t: []) if l == 0 else (lambda t: [("xo", t)])

        P.barrier()
        A = Alloc(nc, BASE)
        modbc = A.t("modbc", [128, 6 * D], F32)
        wst = [A.t("wst%d" % i, [128, 8, 512], F32) for i in range(3)]
        dtmp = A.t("dtmp", [128, 4, 8, 128], F32)
        cab = A.t("cab", [128, 8, 128], F32)
        lamv = A.t("lamv", [128, 128], F32)
        lamp = A.t("lamp", [128, 64], F32)
        for j in range(8):
            op("dve", "tensor_scalar", reads=["ca", "ident"], writes=["cab"], out=cab[:, j, :], in0=ident[:, :],
               scalar1=0.0, scalar2=ca[:, j:j + 1], op0=ALU.mult, op1=ALU.add)
        ld(modbc[:], b_ada[l:l + 1, :].partition_broadcast(128), "m0", ["modbc"])
        wada_v = w_ada[l].rearrange("(j p) n -> p j n", p=128)
        for n in range(12):
            sl = n % 3
            ld(wst[sl][:], wada_v[:, :, n * 512:(n + 1) * 512], ("wst", sl), [("wst", sl)])
            for j in range(8):
                op("pe", "matmul", reads=["cab", ("wst", sl)], writes=bk(n % 2),
                   out=bank(n % 2), lhsT=cab[:, j, :], rhs=wst[sl][:, j, :], start=(j == 0), stop=(j == 7))
            op("dve", "tensor_tensor", reads=bk(n % 2) + ["modbc"], writes=["modbc"],
               out=modbc[:, n * 512:(n + 1) * 512], in0=modbc[:, n * 512:(n + 1) * 512], in1=bank(n % 2), op=ALU.add)
        op("dve", "tensor_copy", reads=["modbc"], writes=["GT0"], out=GT[:, 0, :], in_=modbc[:, 2 * D:3 * D])
        op("dve", "tensor_copy", reads=["modbc"], writes=["GT1"], out=GT[:, 1, :], in_=modbc[:, 5 * D:6 * D])
        for k, off in enumerate((0, D, 3 * D, 4 * D)):
            op("dve", "tensor_tensor", reads=["modbc", "ident"], writes=["dtmp"],
               out=dtmp[:, k, :, :], in0=modbc[:, off:off + D].rearrange("p (j m) -> p j m", m=128),
               in1=ident[:, :].unsqueeze(1).to_broadcast([128, 8, 128]), op=ALU.mult)
        for k, dst in ((0, 1), (1, 0), (2, 3), (3, 2)):
            op("dve", "tensor_reduce", reads=["dtmp"], writes=[("mcols", dst)],
               out=mcols[:, dst, :], in_=dtmp[:, k, :, :], axis=AX.X, op=ALU.add)
        for dst, goff in ((0, 0), (2, 8)):
            op("dve", "scalar_tensor_tensor", reads=[("mcols", dst), "gcol"], writes=[("mcols", dst)],
               out=mcols[:, dst, :], in0=mcols[:, dst, :], scalar=1.0,
               in1=gcol_t[:, l * 16 + goff:l * 16 + goff + 8], op0=ALU.add, op1=ALU.mult)
        ld(lamv[:], lam_in[l:l + 1, :].partition_broadcast(128), "m1", ["lamv"])
        lv = lamv[:].rearrange("p (a t b) -> p a t b", a=2, t=2)
        op("dve", "tensor_tensor", reads=["lamv"], writes=["lamp"], out=lamp[:].rearrange("p (a b) -> p a b", a=2),
           in0=lv[:, :, 0, :], in1=lv[:, :, 1, :], op=ALU.mult)
        op("dve", "tensor_reduce", reads=["lamp"], writes=["lamt23"], out=lamt[:, 2:4],
           in_=lamp[:].rearrange("p (a b) -> p a b", a=2), axis=AX.X, op=ALU.add)
        op("act", "activation", reads=["lamt23"], writes=["lamt45"], out=lamt[:, 4:6], in_=lamt[:, 2:4], func=AF.Exp)
        op("dve", "tensor_tensor", reads=["lamt45"], writes=["lamt0"], out=lamt[:, 0:1], in0=lamt[:, 4:5],
           in1=lamt[:, 5:6], op=ALU.subtract)
        op("dve", "tensor_scalar", reads=["lamt0"], writes=["lamt1"], out=lamt[:, 1:2], in0=lamt[:, 0:1],
           scalar1=lambda_init, scalar2=-1.0, op0=ALU.add, op1=ALU.mult)
        op("pool", "memset", writes=["cst1"], ap=cst[:, 1:2], constant=math.log(1.0 - lambda_init))

        P.barrier()
        A = Alloc(nc, BASE)
        QT = A.t("QT", [128, 4, S], BF16)
        KT = A.t("KT", [128, 4, S], BF16)
        V = A.t("V", [128, NT, NH, 128], BF16)
        A_att = A.off
        Wb = A.t("Wb", [128, 8, INW], BF16)
        A_tok = A.off
        stg = [A.t("stg%d" % i, [128, INW], F32) for i in range(2)]
        A = Alloc(nc, A_tok)
        xt = [A.t("xt%d" % i, [128, D], F32) for i in range(2)]
        xn = A.t("xn", [128, D], F32)
        hT = A.t("hT", [128, 8, 512], BF16)
        pst = [A.t("pst%d" % i, [128, 512], F32) for i in range(3)]
        ssq = A.t("ssq", [128, 4], F32)

        win_v = w_in[l].rearrange("(j p) n -> p j n", p=128)
        for j in range(8):
            sl = j % 2
            ld(stg[sl][:], win_v[:, j, :], ("stg", sl), [("stg", sl)])
            op("dve" if j % 2 == 0 else "pool", "tensor_copy", reads=[("stg", sl)], writes=[("Wb", j)],
               out=Wb[:, j, :], in_=stg[sl][:])
        op("pool", "memset", writes=["Vones"], ap=V[:, :, :, 64:128], constant=1.0)
        P.barrier()

        cnt = 0
        for i in range(NS):
            for tt in range(4):
                t = i * 4 + tt
                slot = t % 2
                ld(xt[slot][:], x_src[t * 128:(t + 1) * 128, :], ("xt", slot), [("xt", slot)], rkeys=xkeys(t))
                norm_tile(xt[slot], ("xt", slot), xn, ssq, hT, tt, 0, 1, (t % 2) * 2)
            for m in range(16):
                colbase = m * 128 if m < 8 else 1536 + (m - 8) * 128
                b = 4 + (cnt % 4)
                cnt += 1
                for j in range(8):
                    op("pe", "matmul", reads=[("Wb", j), "hT"], writes=bk(b), out=bank(b),
                       lhsT=Wb[:, j, colbase:colbase + 128], rhs=hT[:, j, :], start=(j == 0), stop=(j == 7))
                if m < 4:
                    op("act", "activation", reads=bk(b), writes=[("QT", i)], out=QT[:, m, i * 512:(i + 1) * 512],
                       in_=bank(b), func=AF.Copy, scale=32.0 ** -0.5)
                elif m < 8:
                    op("dve", "tensor_copy", reads=bk(b), writes=[("KT", i)], out=KT[:, m - 4, i * 512:(i + 1) * 512],
                       in_=bank(b))
                else:
                    s3 = m % 3
                    if m % 2 == 0:
                        op("act", "activation", reads=bk(b), writes=[("pst", s3)], out=pst[s3][:], in_=bank(b), func=AF.Copy)
                    else:
                        op("dve", "tensor_copy", reads=bk(b), writes=[("pst", s3)], out=pst[s3][:], in_=bank(b))
                    st(pc[m - 8, :, 8 + i * 512:8 + (i + 1) * 512], pst[s3][:], ("pst", s3), [("pst", s3)], [("pc", m - 8)])
            for tt in range(4):
                t = i * 4 + tt
                b = 4 + (cnt % 4)
                cnt += 1
                for j in range(8):
                    op("pe", "matmul", reads=[("Wb", j), "hT"], writes=bk(b), out=bank(b),
                       lhsT=hT[:, j, tt * 128:(tt + 1) * 128], rhs=Wb[:, j, 1024:1536], start=(j == 0), stop=(j == 7))
                if tt % 2 == 0:
                    op("act", "activation", reads=bk(b), writes=[("V", t)], out=V[:, t, :, 0:64],
                       in_=bank(b).rearrange("p (h d) -> p h d", d=64), func=AF.Copy)
                else:
                    op("dve", "tensor_copy", reads=bk(b), writes=[("V", t)], out=V[:, t, :, 0:64],
                       in_=bank(b).rearrange("p (h d) -> p h d", d=64))

        P.barrier()
        A = Alloc(nc, A_att)
        qrel = A.t("qrel", [128, 512], BF16)
        alL = A.t("alL", [128, NH, 128], BF16)
        alR = A.t("alR", [128, NH, 128], BF16)
        btab = A.t("btab", [128, NH, NB], F32)
        dtab = A.t("dtab", [128, 4, 512], F32)
        PT = [A.t("PT%d" % i, [128, 1024], BF16) for i in range(2)]
        Osb = A.t("Osb", [128, 1024], F32)
        Tm = A.t("Tm", [64, 1024], F32)
        att = A.t("att", [64, 512], F32)
        sqb = A.t("sqb", [64, 512], F32)
        rsb = A.t("rsb", [64, 512], F32)
        ya = [A.t("ya%d" % i, [64, 512], BF16) for i in range(2)]
        qrt = A.t("qrt", [128, 512], F32)
        NG = 2
        wst4 = [A.t("wst4_%d" % i, [128, DFF], F32) for i in range(NG)]
        wbo = [A.t("wbo%d" % i, [128, DFF], BF16) for i in range(2)]
        ld(qrt[:], c_qr, "c11", ["qrt"])
        ld(qrel[:], c_qrel, "c1", ["qrel"])
        ld(alL[:], c_alL.rearrange("p (h k) -> p h k", h=NH), "c2", ["alL"])
        ld(alR[:], c_alR.rearrange("p (h k) -> p h k", h=NH), "c3", ["alR"])
        ld(btab[:], c_bt.rearrange("p (h k) -> p h k", h=NH), "c4", ["btab"])
        ld(dtab[:], c_dt.rearrange("p (o q) -> p o q", o=4), "c5", ["dtab"])
        def keep(h, i, j):
            dmin = max(0, j * 128 - (i * 512 + 511), i * 512 - (j * 128 + 127))
            return ms[h] * dmin < SKIP_BIAS
        groups = [(h, i, j) for h in range(NH) for i in range(NS) for j in range(NT) if keep(h, i, j)]
        first_j = {}
        last_j = {}
        for (h_, i_, j_) in groups:
            first_j.setdefault((h_, i_), j_)
            last_j[(h_, i_)] = j_
        pending = []

        def emit_scores(n):
            h, i, j = groups[n]
            cch, rb, o = h // 2, (h % 2) * 64, j - 4 * i
            SB = 2 * (n % 2)
            diag = 0 <= o <= 3
            dve_bias = diag or (n % DVE_BIAS_MOD == 0)
            for p in range(2):
                r0 = rb + 32 * p
                op("pe", "matmul", reads=[("KT", j // 4), ("QT", i)], writes=bk(SB + p), out=bank(SB + p),
                   lhsT=KT[r0:r0 + 32, cch, j * 128:(j + 1) * 128], rhs=QT[r0:r0 + 32, cch, i * 512:(i + 1) * 512],
                   start=True, stop=dve_bias, tile_position=(r0, 0))
            if dve_bias and not diag:
                sgn = -ms[h] if o < 0 else ms[h]
                for p in range(2):
                    op("dve", "scalar_tensor_tensor", reads=["qrt"] + bk(SB + p), writes=bk(SB + p), out=bank(SB + p),
                       in0=qrt[:], scalar=sgn, in1=bank(SB + p), op0=ALU.mult, op1=ALU.add)
            elif not diag:
                al = alL if o < 0 else alR
                for p in range(2):
                    r0 = rb + 32 * p
                    op("pe", "matmul", reads=["alL", "alR", "qrel"], writes=bk(SB + p), out=bank(SB + p),
                       lhsT=al[r0:r0 + 2, h, :], rhs=qrel[r0:r0 + 2, :], start=False, stop=True, tile_position=(r0, 0))
            else:
                for p in range(2):
                    op("dve", "scalar_tensor_tensor", reads=["dtab"] + bk(SB + p), writes=bk(SB + p), out=bank(SB + p),
                       in0=dtab[:, o, :], scalar=-ms[h], in1=bank(SB + p), op0=ALU.mult, op1=ALU.add)

        def emit_exp(n):
            h, i, j = groups[n]
            o = j - 4 * i
            sb2 = n % 2
            op("act", "activation", reads=bk(2 * sb2, 2) + ["btab"], writes=[("PT", sb2)], out=PT[sb2][:], in_=bank(2 * sb2, 2),
               func=AF.Exp, bias=btab[:, h, o + NT:o + NT + 1], scale=1.0)

        def emit_pv(n):
            h, i, j = groups[n]
            sb2 = n % 2
            for p in range(2):
                op("pe", "matmul", reads=[("V", j), "Vones", ("PT", sb2)], writes=bk(4 + p),
                   out=ps_all[:, (4 + p) * 512:(5 + p) * 512], lhsT=V[:, j, h, :],
                   rhs=PT[sb2][:, p * 512:(p + 1) * 512], start=(j == first_j[(h, i)]), stop=(j == last_j[(h, i)]))

        def epilogue_stages(h, i):
            ys = (h * NS + i) % 2

            def s0():
                op("dve", "tensor_copy", reads=bk(4, 2), writes=["Osb"], out=Osb[0:65, :], in_=ps_all[0:65, 2048:3072])
                op("dve", "reciprocal", reads=["Osb"], writes=["Osb"], out=Osb[64:65, :], in_=Osb[64:65, :])
                op("dve", "tensor_scalar", reads=["Osb", "lamt1"], writes=["Osb"], out=Osb[64:65, 512:1024],
                   in0=Osb[64:65, 512:1024], scalar1=lamt[64:65, 1:2], scalar2=None, op0=ALU.mult)

            def s1():
                for p in range(2):
                    op("pe", "matmul", reads=["Osb", "sel"], writes=bk(6 + p), out=ps_all[0:64, (6 + p) * 512:(7 + p) * 512],
                       lhsT=sel[0:65, 0:64], rhs=Osb[0:65, p * 512:(p + 1) * 512], start=True, stop=True)

            def s2():
                op("dve", "tensor_tensor", reads=["Osb"] + bk(6, 2), writes=["Tm"], out=Tm[:], in0=Osb[0:64, :],
                   in1=ps_all[0:64, 3072:4096], op=ALU.mult)
                op("dve", "tensor_tensor", reads=["Tm"], writes=["att"], out=att[:], in0=Tm[:, 0:512], in1=Tm[:, 512:1024], op=ALU.add)

            def s3():
                op("act", "activation", reads=["att"], writes=["sqb"], out=sqb[:], in_=att[:], func=AF.Square)

            def s4():
                op("pe", "matmul", reads=["sqb", "ones32"], writes=bk(6), out=ps_all[0:64, 3072:3584], lhsT=ones32[0:64, 0:64],
                   rhs=sqb[:], start=True, stop=True)

            def s5():
                rstd_ops(ps_all[0:64, 3072:3584], rsb[:], ("ps", 6), "rsb", 1.0 / 64, npart=64, extra_bias=cst[0:64, 1:2])

            def s6():
                op("dve", "scalar_tensor_tensor", reads=["att", "rsb", "gsubraw"], writes=[("ya", ys)], out=ya[ys][:], in0=att[:],
                   scalar=gsub_t[:, l * 8 + h:l * 8 + h + 1], in1=rsb[:], op0=ALU.mult, op1=ALU.mult)
                st(yts[h * 64:(h + 1) * 64, i * 512:(i + 1) * 512], ya[ys][:], ("ya", ys), [("ya", ys)], [("yts", h // 2)])

            return [s0, s1, s2, s3, s4, s5, s6]

        wgu_v = w_gu[l].rearrange("(j p) n -> p j n", p=128)
        wd_v = w_down[l].rearrange("(f p) n -> p f n", p=128)
        wgub_v = wgu_bf.rearrange("(j p) n -> p j n", p=128)
        wdb_v = wd_bf.rearrange("(f p) n -> p f n", p=128)
        witems = [("gu", j_, half_) for j_ in range(8) for half_ in range(2)] + [("d", f_, 0) for f_ in range(0, 22, 2)]
        wstate = {"k": 0}

        def emit_witem(item):
            k_ = wstate["k"]
            wstate["k"] += 1
            sl, so = k_ % NG, k_ % 2
            if item[0] == "gu":
                _, j_, half_ = item
                ld(wst4[sl][:], wgu_v[:, j_, half_ * DFF:(half_ + 1) * DFF], ("wst4", sl), [("wst4", sl)])
                op("pool", "tensor_copy", reads=[("wst4", sl)], writes=[("wbo", so)], out=wbo[so][:], in_=wst4[sl][:])
                st(wgub_v[:, j_, half_ * DFF:(half_ + 1) * DFF], wbo[so][:], ("wbo", so), [("wbo", so)], [("wgubf", j_)])
            else:
                _, f_, _ = item
                ld(wst4[sl][:, 0:2 * D].rearrange("p (f n) -> p f n", f=2), wd_v[:, f_:f_ + 2, :], ("wst4", sl), [("wst4", sl)])
                op("pool", "tensor_tensor", reads=[("wst4", sl), "GT1"], writes=[("wbo", so)],
                   out=wbo[so][:, 0:2 * D].rearrange("p (f n) -> p f n", f=2),
                   in0=wst4[sl][:, 0:2 * D].rearrange("p (f n) -> p f n", f=2),
                   in1=GT[:, 1:2, :].to_broadcast([128, 2, D]), op=ALU.mult)
                st(wdb_v[:, f_:f_ + 2, :], wbo[so][:, 0:2 * D].rearrange("p (f n) -> p f n", f=2), ("wbo", so), [("wbo", so)],
                   [("wdbf", f_ // 2)])

        wgap = max(1, len(groups) // (len(witems) + 2))
        emit_scores(0)
        for n in range(len(groups)):
            h, i, j = groups[n]
            if witems and n % wgap == wgap - 1:
                emit_witem(witems.pop(0))
            emit_exp(n)
            if n + 1 < len(groups):
                emit_scores(n + 1)
            emit_pv(n)
            if j == last_j[(h, i)]:
                while pending:
                    pending.pop(0)()
                stg_list = epilogue_stages(h, i)
                stg_list[0]()
                pending = stg_list[1:]
            elif pending and (j % EPI_GAP == EPI_GAP - 1 or NT <= 8):
                pending.pop(0)()
        while pending:
            pending.pop(0)()
        while witems:
            emit_witem(witems.pop(0))

        P.barrier()
        A = Alloc(nc, BASE)
        fb = [A.t("fb%d" % i, [128, SP], F32) for i in range(4)]
        ob = A.t("ob", [128, S], BF16)
        wpbd = A.t("wpbd", [128, 128], F32)
        wpb = A.t("wpb", [128, 128], BF16)
        yb = [A.t("yb%d" % i, [128, 512], BF16) for i in range(2)]
        pco = l * 8
        FB = lambda k: ("fb", k)
        for ci in range(2):
            ld(fb[0][:], pc[4 + ci], ("fb", 0), [FB(0)], rkeys=[("pc", 4 + ci), ("pcpad", 4 + ci)])
            ld(fb[1][:], pc[6 + ci], ("fb", 1), [FB(1)], rkeys=[("pc", 6 + ci), ("pcpad", 6 + ci)])
            ld(fb[2][:], pc[2 + ci], ("fb", 2), [FB(2)], rkeys=[("pc", 2 + ci), ("pcpad", 2 + ci)])
            op("dve", "tensor_tensor", reads=[FB(0), FB(1)], writes=[FB(0)], out=fb[0][:], in0=fb[0][:], in1=fb[1][:], op=ALU.mult)
            wc = [pcol_t[:, pco + 2 + jj * 2 + ci:pco + 2 + jj * 2 + ci + 1] for jj in range(3)]
            op("dve", "tensor_scalar", reads=[FB(0), "pcol"], writes=[FB(1)], out=fb[1][:, 8:8 + S], in0=fb[0][:, 7:7 + S],
               scalar1=wc[0], scalar2=None, op0=ALU.mult)
            op("dve", "scalar_tensor_tensor", reads=[FB(0), FB(1), "pcol"], writes=[FB(3)], out=fb[3][:, 8:8 + S],
               in0=fb[0][:, 8:8 + S], scalar=wc[1], in1=fb[1][:, 8:8 + S], op0=ALU.mult, op1=ALU.add)
            op("dve", "scalar_tensor_tensor", reads=[FB(0), FB(3), "pcol"], writes=[FB(1)], out=fb[1][:, 8:8 + S],
               in0=fb[0][:, 9:9 + S], scalar=wc[2], in1=fb[3][:, 8:8 + S], op0=ALU.mult, op1=ALU.add)
            op("dve", "tensor_tensor", reads=[FB(1), FB(2)], writes=["ob"], out=ob[:], in0=fb[1][:, 8:8 + S],
               in1=fb[2][:, 8:8 + S], op=ALU.mult)
            st(yts[768 + ci * 128:768 + (ci + 1) * 128, :], ob[:], "ob", ["ob"], [("yts", 6 + ci)])
        for ci in range(2):
            ld(fb[0][:], pc[ci], ("fb", 0), [FB(0)], rkeys=[("pc", ci), ("pcpad", ci)])
            op("pool", "memset", writes=["wpbd"], ap=wpbd[:], constant=0.0)
            for g2 in range(2):
                ld(wpbd[g2 * 64:(g2 + 1) * 64, g2 * 64:(g2 + 1) * 64], w_pool[l, ci * 2 + g2], ("wp", g2), ["wpbd"])
            op("dve", "tensor_copy", reads=["wpbd"], writes=["wpb"], out=wpb[:], in_=wpbd[:])
            u, p2, p4, p8 = fb[0], fb[1], fb[2], fb[3]
            op("dve", "tensor_tensor", reads=[FB(0)], writes=[FB(1)], out=p2[:, 0:SP - 1], in0=u[:, 0:SP - 1], in1=u[:, 1:SP], op=ALU.add)
            if ci == 0:
                op("dve", "tensor_copy", reads=[FB(1)], writes=[FB(2)], out=fb[2][0:64, 8:8 + S], in_=p2[0:64, 7:7 + S])
                op("dve", "tensor_tensor", reads=[FB(1)], writes=[FB(2)], out=fb[2][64:128, 8:8 + S],
                   in0=p2[64:128, 6:6 + S], in1=p2[64:128, 8:8 + S], op=ALU.add)
                sb_, skey = fb[2], FB(2)
            else:
                op("dve", "tensor_tensor", reads=[FB(1)], writes=[FB(2)], out=p4[:, 0:SP - 3], in0=p2[:, 0:SP - 3],
                   in1=p2[:, 2:SP - 1], op=ALU.add)
                op("dve", "tensor_tensor", reads=[FB(2)], writes=[FB(3)], out=p8[64:128, 0:SP - 7], in0=p4[64:128, 0:SP - 7],
                   in1=p4[64:128, 4:SP - 3], op=ALU.add)
                op("dve", "tensor_tensor", reads=[FB(2), FB(1)], writes=[FB(1)], out=fb[1][0:64, 8:8 + S],
                   in0=p4[0:64, 4:4 + S], in1=p4[0:64, 8:8 + S], op=ALU.add)
                op("dve", "tensor_tensor", reads=[FB(3), FB(1)], writes=[FB(1)], out=fb[1][64:128, 8:8 + S],
                   in0=p8[64:128, 0:S], in1=p8[64:128, 8:8 + S], op=ALU.add)
                sb_, skey = fb[1], FB(1)
            op("dve", "tensor_scalar", reads=[skey, "poolc"], writes=[skey], out=sb_[:, 8:8 + S], in0=sb_[:, 8:8 + S],
               scalar1=poolc[:, ci, 0:1], scalar2=None, op0=ALU.mult)
            op("dve", "tensor_tensor", reads=[skey, "poolc"], writes=[skey], out=sb_[:, 8:16], in0=sb_[:, 8:16],
               in1=poolc[:, ci, 1:9], op=ALU.mult)
            op("dve", "tensor_tensor", reads=[skey, "poolc"], writes=[skey], out=sb_[:, S:S + 8], in0=sb_[:, S:S + 8],
               in1=poolc[:, ci, 9:17], op=ALU.mult)
            op("dve", "tensor_tensor", reads=[skey, FB(0)], writes=["ob"], out=ob[:], in0=sb_[:, 8:8 + S], in1=u[:, 8:8 + S],
               op=ALU.subtract)
            for i in range(NS):
                b = i % 4
                op("pe", "matmul", reads=["wpb", "ob"], writes=bk(b), out=bank(b), lhsT=wpb[:], rhs=ob[:, i * 512:(i + 1) * 512],
                   start=True, stop=True)
                ys = i % 2
                op("act", "activation", reads=bk(b) + ["pcol"], writes=[("yb", ys)], out=yb[ys][:], in_=bank(b),
                   func=AF.Identity, scale=pcol_t[:, pco + ci:pco + ci + 1])
                st(yts[512 + ci * 128:512 + (ci + 1) * 128, i * 512:(i + 1) * 512], yb[ys][:], ("yb", ys), [("yb", ys)],
                   [("yts", 4 + ci)])

        P.barrier()
        A = Alloc(nc, BASE)
        Wo = A.t("Wo", [128, 8, D], BF16)
        wos = [A.t("wos%d" % i, [128, D], F32) for i in range(2)]
        Yt = [A.t("Yt%d" % i, [128, 8, 512], BF16) for i in range(2)]
        xb = [A.t("xb%d" % i, [128, D], F32) for i in range(2)]
        wout_v = w_out[l].rearrange("(j p) n -> p j n", p=128)
        for j in range(8):
            sl = j % 2
            ld(wos[sl][:], wout_v[:, j, :], ("wos", sl), [("wos", sl)])
            op("dve", "tensor_tensor", reads=[("wos", sl), "GT0"], writes=[("Wo", j)], out=Wo[:, j, :], in0=wos[sl][:],
               in1=GT[:, 0, :], op=ALU.mult)
        yts_v = yts.rearrange("(c p) s -> p c s", p=128)
        for i in range(NS):
            ysl = i % 2
            ld(Yt[ysl][:], yts_v[:, :, i * 512:(i + 1) * 512], ("Yt", ysl), [("Yt", ysl)], rkeys=[("yts", c8) for c8 in range(8)])
            for tt in range(4):
                t = i * 4 + tt
                slot = t % 2
                ld(xb[slot][:], x_src[t * 128:(t + 1) * 128, :], ("xb", slot), [("xb", slot)], rkeys=xkeys(t))
                pb = (t % 2) * 2
                for half in range(2):
                    for c8 in range(8):
                        op("pe", "matmul", reads=[("Yt", ysl), ("Wo", c8)], writes=bk(pb + half), out=bank(pb + half),
                           lhsT=Yt[ysl][:, c8, tt * 128:(tt + 1) * 128], rhs=Wo[:, c8, half * 512:(half + 1) * 512],
                           start=(c8 == 0), stop=(c8 == 7))
                op("dve", "tensor_tensor", reads=[("xb", slot)] + bk(pb, 2), writes=[("xb", slot)], out=xb[slot][:],
                   in0=xb[slot][:], in1=bank(pb, 2), op=ALU.add)
                st(out[t * 128:(t + 1) * 128, :], xb[slot][:], ("xbs", slot), [("xb", slot)], [("xo", t)])

        P.barrier()
        A = Alloc(nc, BASE)
        Wgu = A.t("Wgu", [128, 8, 2 * DFF], BF16)
        Wd = A.t("Wd", [128, 22, D], BF16)
        X1 = [A.t("X1_%d" % i, [128, D], F32) for i in range(4)]
        xn2 = A.t("xn2", [128, D], F32)
        hT2 = A.t("hT2", [128, 8, 512], BF16)
        hT3 = A.t("hT3", [128, 8, 512], BF16)
        sgall = A.t("sgall", [128, 2, 512], F32)
        sg = [sgall[:, 0, :], sgall[:, 1, :]]
        fsq = sgall[:].rearrange("p a b -> p (a b)")
        ssq2 = A.t("ssq2", [128, 4], F32)
        A_h = A.off
        Hh = A.t("Hh", [128, 22, 512], BF16)
        for j in range(8):
            ld(Wgu[:, j, :], wgub_v[:, j, :], ("wgl", j % 4), [("Wgu", j)], rkeys=[("wgubf", j)])
        for f in range(0, 22, 2):
            ld(Wd[:, f:f + 2, :], wdb_v[:, f:f + 2, :], ("wdl", (f // 2) % 4), [("Wd", f), ("Wd", f + 1)], rkeys=[("wdbf", f // 2)])
        P.barrier()
        gfb = xn2
        if last:
            A = Alloc(nc, A_h + 22 * 512 * 2)
            gfb = A.t("gfb", [128, D], F32)
            ld(gfb[:], gfin[0:1, :].partition_broadcast(128), "gf", ["gfb"])
        gi = 0
        xa = X1[0:2]
        xr = X1[2:4]
        hTb = [hT2, hT3]

        def load_norm_in(i_, tt_):
            t_ = i_ * 4 + tt_
            ld(xa[t_ % 2][:], out[t_ * 128:(t_ + 1) * 128, :], ("xa", t_ % 2), [("xa", t_ % 2)], rkeys=[("xo", t_)])

        for tt in range(4):
            load_norm_in(0, tt)
            norm_tile(xa[tt % 2], ("xa", tt % 2), xn2, ssq2, hTb[0], tt, 2, 3, 0, hkey=("hT", 0))
        for i in range(NS):
            hcur = hTb[i % 2]
            hk = ("hT", i % 2)
            nxt = i + 1 < NS
            for f in range(22):
                gb = 2 + 2 * (gi % 2)
                sgs = gi % 2
                gi += 1
                for which in range(2):
                    cb = which * DFF + f * 128
                    for j in range(8):
                        op("pe", "matmul", reads=[("Wgu", j), hk], writes=bk(gb + which), out=bank(gb + which),
                           lhsT=Wgu[:, j, cb:cb + 128], rhs=hcur[:, j, :], start=(j == 0), stop=(j == 7))
                op("act", "activation", reads=bk(gb), writes=[("sg", sgs)], out=sg[sgs][:], in_=bank(gb), func=AF.Silu)
                op("dve", "tensor_tensor", reads=[("sg", sgs)] + bk(gb + 1), writes=["Hh"], out=Hh[:, f, :], in0=sg[sgs][:],
                   in1=bank(gb + 1), op=ALU.mult)
                if nxt and f % 5 == 1 and f // 5 < 4:
                    tt_ = f // 5
                    load_norm_in(i + 1, tt_)
                    norm_p1(xa[(i * 4 + 4 + tt_) % 2], ("xa", (i * 4 + 4 + tt_) % 2), xn2, ssq2)
                if nxt and f % 5 == 4 and f // 5 < 4:
                    norm_p2(xn2, hTb[(i + 1) % 2], f // 5, 2, 3, 0, hkey=("hT", (i + 1) % 2))
            for tt in range(4):
                t = i * 4 + tt
                rs = t % 2
                ld(xr[rs][:], out[t * 128:(t + 1) * 128, :], ("xr", rs), [("xr", rs)], rkeys=[("xo", t)])
                for half in range(2):
                    for f in range(22):
                        op("pe", "matmul", reads=["Hh", ("Wd", f)], writes=bk(6 + half), out=bank(6 + half),
                           lhsT=Hh[:, f, tt * 128:(tt + 1) * 128], rhs=Wd[:, f, half * 512:(half + 1) * 512],
                           start=(f == 0), stop=(f == 21))
                op("dve", "tensor_tensor", reads=[("xr", rs)] + bk(6, 2), writes=[("xr", rs)], out=xr[rs][:], in0=xr[rs][:],
                   in1=bank(6, 2), op=ALU.add)
                if last:
                    op("act", "activation", reads=[("xr", rs)], writes=[("sg", 0), ("sg", 1), "ssq2"], out=fsq, in_=xr[rs][:],
                       func=AF.Square, accum_out=ssq2[:, 2:3])
                    rstd_ops(ssq2[:, 2:3], ssq2[:, 3:4], "ssq2", "ssq3", 1.0 / D)
                    op("dve", "scalar_tensor_tensor", reads=[("xr", rs), "ssq3", "gfb"], writes=[("xr", rs)], out=xr[rs][:],
                       in0=xr[rs][:], scalar=ssq2[:, 3:4], in1=gfb[:], op0=ALU.mult, op1=ALU.mult)
                st(out[t * 128:(t + 1) * 128, :], xr[rs][:], ("x1s", rs), [("xr", rs)], [("xo", t)])
    P.emit()
    return nc, P


def make_consts(S):
    NT = S // 128
    NB = 2 * NT
    bf = ml_dtypes.bfloat16
    ms = slopes()
    ident = np.eye(128, dtype=np.float32)
    q = np.arange(512)
    qrel = np.zeros((128, 512), np.float32)
    for r in range(4):
        qrel[32 * r + 0] = q % 256
        qrel[32 * r + 1] = 256 * (q >= 256)
    alL = np.zeros((128, NH, 128), np.float32)
    alR = np.zeros((128, NH, 128), np.float32)
    for h in range(NH):
        rb = (h % 2) * 64
        for p in range(2):
            for rr in range(2):
                alL[rb + 32 * p + rr, h, :] = -ms[h]
                alR[rb + 32 * p + rr, h, :] = ms[h]
    krel = np.arange(128, dtype=np.float64)
    bt = np.zeros((128, NH, NB), np.float64)
    for h in range(NH):
        for o in range(-NT + 1, NT):
            if o < 0:
                bt[:, h, o + NT] = ms[h] * krel + 128.0 * ms[h] * o
            elif o > 3:
                bt[:, h, o + NT] = -ms[h] * krel - 128.0 * ms[h] * o
    dtb = np.zeros((128, 4, 512), np.float32)
    for o in range(4):
        dtb[:, o, :] = np.abs(q[None, :] - krel[:, None] - 128 * o)
    poolc = np.zeros((128, 2, 17), np.float32)
    wins = (2, 4, 8, 16)
    for ci in range(2):
        for g2 in range(2):
            w = wins[ci * 2 + g2]
            rows = slice(g2 * 64, (g2 + 1) * 64)
            poolc[rows, ci, 0] = 1.0 / w
            for e_ in range(8):
                for side, t in ((0, e_), (1, S - 8 + e_)):
                    lo = min(max(t - w // 2, 0), S)
                    hi = min(max(t + w - w // 2, 0), S)
                    poolc[rows, ci, 1 + side * 8 + e_] = float(w) / float(hi - lo)
    return {
        "c_ident": ident,
        "c_qrel": qrel.astype(bf),
        "c_alL": alL.reshape(128, NH * 128).astype(bf),
        "c_alR": alR.reshape(128, NH * 128).astype(bf),
        "c_bt": bt.reshape(128, NH * NB).astype(np.float32),
        "c_dt": dtb.reshape(128, 4 * 512),
        "c_pool": poolc.reshape(128, 34),
        "c_qr": np.ascontiguousarray(np.broadcast_to(q.astype(np.float32)[None, :], (128, 512))),
    }


def col128(v):
    v = np.asarray(v, np.float32)
    return np.ascontiguousarray(v.reshape(-1, 128).T)


def make_in_maps(S, x, c, w_ada, b_ada, g_mix, w_in, lambda_q1, lambda_k1, lambda_q2, lambda_k2,
                 g_subln, w_pool, pool_scale, conv_w, w_out, g_ffn, w_gate_up, w_down, g_final):
    f = lambda a: np.ascontiguousarray(np.asarray(a, np.float32))
    B = x.shape[0]
    consts = make_consts(S)
    gcols = np.concatenate([np.concatenate([col128(g_mix[l]), col128(g_ffn[l])], axis=1) for l in range(DEPTH)], axis=1)
    lam = np.stack([np.concatenate([f(lambda_q1[l]), f(lambda_k1[l]), f(lambda_q2[l]), f(lambda_k2[l])]) for l in range(DEPTH)])
    gsub = np.concatenate([f(g_subln[l]).reshape(NH, 64).T for l in range(DEPTH)], axis=1)
    pcs = []
    for l in range(DEPTH):
        cols = [col128(pool_scale[l])]
        cw = f(conv_w[l])
        for jj in range(3):
            cols.append(col128(cw[jj]))
        pcs.append(np.concatenate(cols, axis=1))
    pcols = np.concatenate(pcs, axis=1)
    shared = {
        "w_ada": f(w_ada), "b_ada": f(b_ada), "gcols": np.ascontiguousarray(gcols), "w_in": f(w_in),
        "lam": np.ascontiguousarray(lam), "gsub": np.ascontiguousarray(gsub), "w_pool": f(w_pool),
        "pcols": np.ascontiguousarray(pcols), "w_out": f(w_out), "w_gate_up": f(w_gate_up), "w_down": f(w_down),
        "gfin": f(g_final).reshape(1, D),
    }
    shared.update(consts)
    maps = []
    xf = f(x)
    cf = f(c)
    for b in range(B):
        m = dict(shared)
        m["x"] = np.ascontiguousarray(xf[b])
        m["cT"] = col128(cf[b])
        maps.append(m)
    return maps


_CACHE = {}


def kernel(**inputs):
    x = np.asarray(inputs["x"])
    B, S, _ = x.shape
    if S not in _CACHE:
        _CACHE[S] = build(S)[0]
    nc = _CACHE[S]
    in_maps = make_in_maps(S, **inputs)
    res = run_bass_kernel_spmd(nc, in_maps, core_ids=list(range(B)))
    return np.stack([np.asarray(r["out"], np.float32) for r in res.results], axis=0)
```

```python
import math
import numpy as np
import ml_dtypes
import concourse.bass as bass
import concourse.mybir as mybir
from concourse.bass_utils import run_bass_kernel_spmd
from contextlib import ExitStack

F32 = mybir.dt.float32
BF16 = mybir.dt.bfloat16
AF = mybir.ActivationFunctionType
ALU = mybir.AluOpType
AX = mybir.AxisListType

D = 1024
NH = 8
DFF = 2816
INW = 2560
DEPTH = 2
EPS = 1e-6
EPOCH = 12000
EPI_GAP = 2
SKIP_BIAS = 150.0
DVE_BIAS_MOD = 10 ** 9
SBUF_BASE = 16512
SBUF_LIMIT = 229376


class Op:
    __slots__ = ("eng", "fn", "deps", "is_dma", "semkey", "sig", "val", "semid")

    def __init__(self, eng, fn, is_dma, semkey):
        self.eng = eng
        self.fn = fn
        self.deps = []
        self.is_dma = is_dma
        self.semkey = semkey
        self.sig = False
        self.val = None
        self.semid = None


class Prog:
    ENGS = ("pe", "act", "dve", "pool", "sp")

    def __init__(self, nc):
        self.nc = nc
        self.ops = {e: [] for e in self.ENGS}
        self.last_writer = {}
        self.readers = {}
        self.last_compute = {}
        self.last_dma = {}
        self.bar_deps = []
        self.bar_gen = 0
        self.eng_gen = {e: 0 for e in self.ENGS}

    def add(self, eng, method, reads=(), writes=(), dma=False, semkey=None, **kw):
        def fn(e, method=method, kw=kw):
            return getattr(e, method)(**kw)
        op = Op(eng, fn, dma, semkey if dma else None)
        if dma and semkey is None:
            op.semkey = ("dmaq", eng)
        deps = {}
        for b in reads:
            w = self.last_writer.get(b)
            if w is not None:
                deps[id(w)] = (w, "RAW")
        for b in writes:
            w = self.last_writer.get(b)
            if w is not None and id(w) not in deps:
                deps[id(w)] = (w, "WAW")
            for r in self.readers.get(b, ()):
                if id(r) not in deps:
                    deps[id(r)] = (r, "WAR")
        if self.eng_gen[eng] < self.bar_gen:
            self.eng_gen[eng] = self.bar_gen
            for d in self.bar_deps:
                deps[id(d)] = (d, "BAR")
        for d, kind in deps.values():
            if d is op:
                continue
            same = (d.eng == eng) and (not d.is_dma) and (not dma)
            if same:
                if eng == "pe" or kind in ("WAR", "BAR"):
                    continue
            op.deps.append(d)
            d.sig = True
        for b in reads:
            self.readers.setdefault(b, []).append(op)
        for b in writes:
            self.last_writer[b] = op
            self.readers[b] = []
        self.ops[eng].append(op)
        if dma:
            self.last_dma[op.semkey] = op
        else:
            self.last_compute[eng] = op
        return op

    def barrier(self):
        self.bar_deps = list(self.last_compute.values()) + list(self.last_dma.values())
        self.bar_gen += 1

    def emit(self):
        nc = self.nc
        finals = list(self.last_compute.values()) + list(self.last_dma.values())
        for d in finals:
            d.sig = True
        counters = {}
        semids = {}
        for e in self.ENGS:
            for op in self.ops[e]:
                if not op.sig:
                    continue
                key = op.semkey if op.is_dma else ("eng", e)
                c = counters.get(key, 0)
                counters[key] = c + 1
                op.semid = (key, c // EPOCH)
                op.val = (c % EPOCH + 1) * (16 if op.is_dma else 1)
                semids[op.semid] = None
        with ExitStack() as st:
            sems = {}
            for i, k in enumerate(semids):
                sems[k] = st.enter_context(nc.semaphore("s%d" % i))
            self.n_sems = len(sems)
            block = st.enter_context(nc.Block())
            engmap = {"pe": block.tensor, "act": block.scalar, "dve": block.vector,
                      "pool": block.gpsimd, "sp": block.sync}

            def make(e):
                def body(eng):
                    waited = {}

                    def wait(d):
                        if waited.get(d.semid, 0) >= d.val:
                            return
                        eng.wait_ge(sems[d.semid], d.val)
                        waited[d.semid] = d.val

                    for op in self.ops[e]:
                        for d in op.deps:
                            wait(d)
                        ins = op.fn(eng)
                        if op.sig:
                            ins.then_inc(sems[op.semid], 16 if op.is_dma else 1)
                    if e == "sp":
                        for d in finals:
                            if d.eng == "sp" and not d.is_dma:
                                continue
                            wait(d)
                return body

            for e in self.ENGS:
                engmap[e](make(e))


class Alloc:
    def __init__(self, nc, base, limit=SBUF_LIMIT):
        self.nc = nc
        self.off = base
        self.limit = limit
        self.n = 0

    def t(self, name, shape, dtype):
        nb = int(np.prod(shape[1:])) * mybir.dt.size(dtype)
        nb = (nb + 31) // 32 * 32
        self.n += 1
        h = self.nc.alloc_sbuf_tensor_at("%s_%d_%d" % (name, self.off, self.n), list(shape), dtype, offset=self.off)
        self.off += nb
        assert self.off <= self.limit, ("SBUF overflow", name, self.off)
        return h


def slopes():
    return [2.0 ** (-(h + 1)) for h in range(NH)]


def build(S):
    NT = S // 128
    NS = S // 512
    NB = 2 * NT
    SP = S + 16
    nc = bass.Bass("TRN2", target_bir_lowering=False)

    def din(name, shape, dtype=F32):
        return nc.dram_tensor(name, list(shape), dtype, kind="ExternalInput").ap()

    x_in = din("x", [S, D])
    cT = din("cT", [128, 8])
    w_ada = din("w_ada", [DEPTH, D, 6 * D])
    b_ada = din("b_ada", [DEPTH, 6 * D])
    gcols = din("gcols", [128, DEPTH * 16])
    w_in = din("w_in", [DEPTH, D, INW])
    lam_in = din("lam", [DEPTH, 128])
    gsub = din("gsub", [64, DEPTH * 8])
    w_pool = din("w_pool", [DEPTH, 4, 64, 64])
    pcols = din("pcols", [128, DEPTH * 8])
    w_out = din("w_out", [DEPTH, D, D])
    w_gu = din("w_gate_up", [DEPTH, D, 2 * DFF])
    w_down = din("w_down", [DEPTH, DFF, D])
    gfin = din("gfin", [1, D])
    c_ident = din("c_ident", [128, 128])
    c_qrel = din("c_qrel", [128, 512], BF16)
    c_alL = din("c_alL", [128, NH * 128], BF16)
    c_alR = din("c_alR", [128, NH * 128], BF16)
    c_bt = din("c_bt", [128, NH * NB])
    c_dt = din("c_dt", [128, 4 * 512])
    c_pool = din("c_pool", [128, 2 * 17])
    c_qr = din("c_qr", [128, 512])
    out = nc.dram_tensor("out", [S, D], F32, kind="ExternalOutput").ap()
    pc = nc.dram_tensor("pc_scr", [8, 128, SP], F32, kind="Internal").ap()
    yts = nc.dram_tensor("yt_scr", [D, S], BF16, kind="Internal").ap()
    wgu_bf = nc.dram_tensor("wgu_bf", [D, 2 * DFF], BF16, kind="Internal").ap()
    wd_bf = nc.dram_tensor("wd_bf", [DFF, D], BF16, kind="Internal").ap()

    P = Prog(nc)
    op = P.add
    ps_all = nc.alloc_psum_tensor("ps_all", [128, 4096], F32)

    def bank(b, n=1):
        return ps_all[:, b * 512:(b + n) * 512]

    def bk(b, n=1):
        return [("ps", b + i) for i in range(n)]

    def ld(dst, src, key, wkeys, rkeys=(), q="sp"):
        return op(q, "dma_start", reads=list(rkeys), writes=list(wkeys), dma=True, semkey=key, out=dst, in_=src)

    def st(dst, src, key, rkeys, wkeys, q="pool"):
        return op(q, "dma_start", reads=list(rkeys), writes=list(wkeys), dma=True, semkey=key, out=dst, in_=src)

    A0 = Alloc(nc, SBUF_BASE)
    ident = A0.t("ident", [128, 128], F32)
    ones32 = A0.t("ones32", [128, 64], F32)
    sel = A0.t("sel", [128, 64], F32)
    poolc = A0.t("poolc", [128, 2, 17], F32)
    gcol_t = A0.t("gcol", [128, DEPTH * 16], F32)
    pcol_t = A0.t("pcol", [128, DEPTH * 8], F32)
    gsub_t = A0.t("gsub", [64, DEPTH * 8], F32)
    cst = A0.t("cst", [128, 8], F32)
    zpad = A0.t("zpad", [128, 8], F32)
    ca = A0.t("ca", [128, 8], F32)
    GT = A0.t("GT", [128, 2, D], F32)
    mcols = A0.t("mcols", [128, 4, 8], F32)
    lamt = A0.t("lamt", [128, 8], F32)
    BASE = A0.off

    ld(ident[:], c_ident, "c0", ["ident"])
    ld(poolc[:], c_pool.rearrange("p (c k) -> p c k", c=2), "c6", ["poolc"])
    ld(gcol_t[:], gcols, "c7", ["gcol"])
    ld(pcol_t[:], pcols, "c8", ["pcol"])
    ld(gsub_t[:], gsub, "c9", ["gsubraw"])
    ld(ca[:], cT, "c10", ["ca"])
    op("pool", "memset", writes=["ones32"], ap=ones32[:], constant=1.0)
    op("pool", "memset", writes=["sel"], ap=sel[:], constant=0.0)
    op("pool", "memset", writes=["sel"], ap=sel[64:65, :], constant=1.0)
    op("pool", "memset", writes=["zpad"], ap=zpad[:], constant=0.0)
    op("pool", "memset", writes=["cst0"], ap=cst[:, 0:1], constant=EPS)
    op("pool", "memset", writes=["cst2"], ap=cst[:, 2:3], constant=0.0)
    for ch in range(8):
        st(pc[ch, :, 0:8], zpad[:], ("zp", 0), ["zpad"], [("pcpad", ch)])
        st(pc[ch, :, S + 8:S + 16], zpad[:], ("zp", 1), ["zpad"], [("pcpad", ch)])
    op("act", "activation", reads=["ca"], writes=["ca"], out=ca[:], in_=ca[:], func=AF.Silu)

    def rstd_ops(ss_ap, rstd_ap, key_ss, key_rstd, scale, npart=128, extra_bias=None):
        op("act", "activation", reads=[key_ss, "cst0"], writes=[key_rstd],
           out=rstd_ap, in_=ss_ap, func=AF.Ln, scale=scale, bias=cst[0:npart, 0:1])
        if extra_bias is None:
            op("act", "activation", reads=[key_rstd], writes=[key_rstd], out=rstd_ap, in_=rstd_ap, func=AF.Exp, scale=-0.5)
        else:
            op("act", "activation", reads=[key_rstd, "cst1"], writes=[key_rstd],
               out=rstd_ap, in_=rstd_ap, func=AF.Exp, scale=-0.5, bias=extra_bias)

    def norm_p1(xsrc, key_x, xn, ssq):
        op("act", "activation", reads=[key_x], writes=["xn", "ssq0"],
           out=xn[:], in_=xsrc[:], func=AF.Square, accum_out=ssq[:, 0:1])
        rstd_ops(ssq[:, 0:1], ssq[:, 1:2], "ssq0", "ssq1", 1.0 / D)
        op("dve", "tensor_scalar", reads=[key_x, "ssq1"], writes=["xn"],
           out=xn[:], in0=xsrc[:], scalar1=ssq[:, 1:2], scalar2=None, op0=ALU.mult)

    def norm_p2(xn, hT_dst, tt, gs_i, sh_i, pbank, hkey="hT"):
        for j in range(8):
            op("pe", "transpose", reads=["xn", "ident"], writes=bk(pbank + j // 4),
               out=ps_all[:, pbank * 512 + j * 128:pbank * 512 + (j + 1) * 128],
               in_=xn[:, j * 128:(j + 1) * 128], identity=ident[:])
        for j in range(8):
            op("dve", "tensor_scalar", reads=bk(pbank + j // 4) + [("mcols", gs_i), ("mcols", sh_i)], writes=[hkey],
               out=hT_dst[:, j, tt * 128:(tt + 1) * 128],
               in0=ps_all[:, pbank * 512 + j * 128:pbank * 512 + (j + 1) * 128],
               scalar1=mcols[:, gs_i, j:j + 1], scalar2=mcols[:, sh_i, j:j + 1], op0=ALU.mult, op1=ALU.add)

    def norm_tile(xsrc, key_x, xn, ssq, hT_dst, tt, gs_i, sh_i, pbank, hkey="hT"):
        norm_p1(xsrc, key_x, xn, ssq)
        norm_p2(xn, hT_dst, tt, gs_i, sh_i, pbank, hkey)

    ms = slopes()
    for l in range(DEPTH):
        lambda_init = 0.8 - 0.6 * math.exp(-0.3 * l)
        last = (l == DEPTH - 1)
        x_src = x_in if l == 0 else out
        xkeys = (lambda t: []) if l == 0 else (lambda t: [("xo", t)])

        P.barrier()
        A = Alloc(nc, BASE)
        modbc = A.t("modbc", [128, 6 * D], F32)
        wst = [A.t("wst%d" % i, [128, 8, 512], F32) for i in range(3)]
        dtmp = A.t("dtmp", [128, 4, 8, 128], F32)
        cab = A.t("cab", [128, 8, 128], F32)
        lamv = A.t("lamv", [128, 128], F32)
        lamp = A.t("lamp", [128, 64], F32)
        for j in range(8):
            op("dve", "tensor_scalar", reads=["ca", "ident"], writes=["cab"], out=cab[:, j, :], in0=ident[:, :],
               scalar1=0.0, scalar2=ca[:, j:j + 1], op0=ALU.mult, op1=ALU.add)
        ld(modbc[:], b_ada[l:l + 1, :].partition_broadcast(128), "m0", ["modbc"])
        wada_v = w_ada[l].rearrange("(j p) n -> p j n", p=128)
        for n in range(12):
            sl = n % 3
            ld(wst[sl][:], wada_v[:, :, n * 512:(n + 1) * 512], ("wst", sl), [("wst", sl)])
            for j in range(8):
                op("pe", "matmul", reads=["cab", ("wst", sl)], writes=bk(n % 2),
                   out=bank(n % 2), lhsT=cab[:, j, :], rhs=wst[sl][:, j, :], start=(j == 0), stop=(j == 7))
            op("dve", "tensor_tensor", reads=bk(n % 2) + ["modbc"], writes=["modbc"],
               out=modbc[:, n * 512:(n + 1) * 512], in0=modbc[:, n * 512:(n + 1) * 512], in1=bank(n % 2), op=ALU.add)
        op("dve", "tensor_copy", reads=["modbc"], writes=["GT0"], out=GT[:, 0, :], in_=modbc[:, 2 * D:3 * D])
        op("dve", "tensor_copy", reads=["modbc"], writes=["GT1"], out=GT[:, 1, :], in_=modbc[:, 5 * D:6 * D])
        for k, off in enumerate((0, D, 3 * D, 4 * D)):
            op("dve", "tensor_tensor", reads=["modbc", "ident"], writes=["dtmp"],
               out=dtmp[:, k, :, :], in0=modbc[:, off:off + D].rearrange("p (j m) -> p j m", m=128),
               in1=ident[:, :].unsqueeze(1).to_broadcast([128, 8, 128]), op=ALU.mult)
        for k, dst in ((0, 1), (1, 0), (2, 3), (3, 2)):
            op("dve", "tensor_reduce", reads=["dtmp"], writes=[("mcols", dst)],
               out=mcols[:, dst, :], in_=dtmp[:, k, :, :], axis=AX.X, op=ALU.add)
        for dst, goff in ((0, 0), (2, 8)):
            op("dve", "scalar_tensor_tensor", reads=[("mcols", dst), "gcol"], writes=[("mcols", dst)],
               out=mcols[:, dst, :], in0=mcols[:, dst, :], scalar=1.0,
               in1=gcol_t[:, l * 16 + goff:l * 16 + goff + 8], op0=ALU.add, op1=ALU.mult)
        ld(lamv[:], lam_in[l:l + 1, :].partition_broadcast(128), "m1", ["lamv"])
        lv = lamv[:].rearrange("p (a t b) -> p a t b", a=2, t=2)
        op("dve", "tensor_tensor", reads=["lamv"], writes=["lamp"], out=lamp[:].rearrange("p (a b) -> p a b", a=2),
           in0=lv[:, :, 0, :], in1=lv[:, :, 1, :], op=ALU.mult)
        op("dve", "tensor_reduce", reads=["lamp"], writes=["lamt23"], out=lamt[:, 2:4],
           in_=lamp[:].rearrange("p (a b) -> p a b", a=2), axis=AX.X, op=ALU.add)
        op("act", "activation", reads=["lamt23"], writes=["lamt45"], out=lamt[:, 4:6], in_=lamt[:, 2:4], func=AF.Exp)
        op("dve", "tensor_tensor", reads=["lamt45"], writes=["lamt0"], out=lamt[:, 0:1], in0=lamt[:, 4:5],
           in1=lamt[:, 5:6], op=ALU.subtract)
        op("dve", "tensor_scalar", reads=["lamt0"], writes=["lamt1"], out=lamt[:, 1:2], in0=lamt[:, 0:1],
           scalar1=lambda_init, scalar2=-1.0, op0=ALU.add, op1=ALU.mult)
        op("pool", "memset", writes=["cst1"], ap=cst[:, 1:2], constant=math.log(1.0 - lambda_init))

        P.barrier()
        A = Alloc(nc, BASE)
        QT = A.t("QT", [128, 4, S], BF16)
        KT = A.t("KT", [128, 4, S], BF16)
        V = A.t("V", [128, NT, NH, 65], BF16)
        A_att = A.off
        Wb = A.t("Wb", [128, 8, INW], BF16)
        A_tok = A.off
        stg = [A.t("stg%d" % i, [128, INW], F32) for i in range(2)]
        A = Alloc(nc, A_tok)
        xt = [A.t("xt%d" % i, [128, D], F32) for i in range(2)]
        xn = A.t("xn", [128, D], F32)
        hT = A.t("hT", [128, 8, 512], BF16)
        hT_b = A.t("hT_b", [128, 8, 512], BF16)
        pst = [A.t("pst%d" % i, [128, 512], F32) for i in range(3)]
        ssq = A.t("ssq", [128, 4], F32)

        win_v = w_in[l].rearrange("(j p) n -> p j n", p=128)
        for j in range(8):
            sl = j % 2
            ld(stg[sl][:], win_v[:, j, :], ("stg", sl), [("stg", sl)])
            op("dve" if j % 2 == 0 else "pool", "tensor_copy", reads=[("stg", sl)], writes=[("Wb", j)],
               out=Wb[:, j, :], in_=stg[sl][:])
        op("pool", "memset", writes=["Vones"], ap=V[:, :, :, 64:65], constant=1.0)
        P.barrier()

        cnt = 0
        hTa = [hT, hT_b]

        def a_load(t_):
            ld(xt[t_ % 2][:], x_src[t_ * 128:(t_ + 1) * 128, :], ("xt", t_ % 2), [("xt", t_ % 2)], rkeys=xkeys(t_))

        for tt in range(4):
            a_load(tt)
            norm_tile(xt[tt % 2], ("xt", tt % 2), xn, ssq, hTa[0], tt, 0, 1, (tt % 2) * 2, hkey=("hTa", 0))
        for i in range(NS):
            hcur = hTa[i % 2]
            hk = ("hTa", i % 2)
            nxt = i + 1 < NS

            def prefetch(c):
                if not nxt or c // 5 >= 4:
                    return
                tt_ = c // 5
                t_ = (i + 1) * 4 + tt_
                if c % 5 == 1:
                    a_load(t_)
                    norm_p1(xt[t_ % 2], ("xt", t_ % 2), xn, ssq)
                elif c % 5 == 4:
                    norm_p2(xn, hTa[(i + 1) % 2], tt_, 0, 1, (t_ % 2) * 2, hkey=("hTa", (i + 1) % 2))

            for m in range(16):
                colbase = m * 128 if m < 8 else 1536 + (m - 8) * 128
                b = 4 + (cnt % 4)
                cnt += 1
                for j in range(8):
                    op("pe", "matmul", reads=[("Wb", j), hk], writes=bk(b), out=bank(b),
                       lhsT=Wb[:, j, colbase:colbase + 128], rhs=hcur[:, j, :], start=(j == 0), stop=(j == 7))
                if m < 4:
                    op("act", "activation", reads=bk(b), writes=[("QT", i)], out=QT[:, m, i * 512:(i + 1) * 512],
                       in_=bank(b), func=AF.Copy, scale=32.0 ** -0.5)
                elif m < 8:
                    op("dve", "tensor_copy", reads=bk(b), writes=[("KT", i)], out=KT[:, m - 4, i * 512:(i + 1) * 512],
                       in_=bank(b))
                else:
                    s3 = m % 3
                    if m % 2 == 0:
                        op("act", "activation", reads=bk(b), writes=[("pst", s3)], out=pst[s3][:], in_=bank(b), func=AF.Copy)
                    else:
                        op("dve", "tensor_copy", reads=bk(b), writes=[("pst", s3)], out=pst[s3][:], in_=bank(b))
                    st(pc[m - 8, :, 8 + i * 512:8 + (i + 1) * 512], pst[s3][:], ("pst", s3), [("pst", s3)], [("pc", m - 8)])
                prefetch(m)
            for tt in range(4):
                t = i * 4 + tt
                b = 4 + (cnt % 4)
                cnt += 1
                for j in range(8):
                    op("pe", "matmul", reads=[("Wb", j), hk], writes=bk(b), out=bank(b),
                       lhsT=hcur[:, j, tt * 128:(tt + 1) * 128], rhs=Wb[:, j, 1024:1536], start=(j == 0), stop=(j == 7))
                if tt % 2 == 0:
                    op("act", "activation", reads=bk(b), writes=[("V", t)], out=V[:, t, :, 0:64],
                       in_=bank(b).rearrange("p (h d) -> p h d", d=64), func=AF.Copy)
                else:
                    op("dve", "tensor_copy", reads=bk(b), writes=[("V", t)], out=V[:, t, :, 0:64],
                       in_=bank(b).rearrange("p (h d) -> p h d", d=64))
                prefetch(16 + tt)

        P.barrier()
        A = Alloc(nc, A_att)
        qrel = A.t("qrel", [128, 512], BF16)
        alL = A.t("alL", [128, NH, 128], BF16)
        alR = A.t("alR", [128, NH, 128], BF16)
        btab = A.t("btab", [128, NH, NB], F32)
        dtab = A.t("dtab", [128, 4, 512], F32)
        PT = [A.t("PT%d" % i, [128, 1024], BF16) for i in range(2)]
        Osb = A.t("Osb", [128, 1024], F32)
        Tm = A.t("Tm", [64, 1024], F32)
        att = A.t("att", [64, 512], F32)
        sqb = A.t("sqb", [64, 512], F32)
        rsb = A.t("rsb", [64, 512], F32)
        ya = [A.t("ya%d" % i, [64, 512], BF16) for i in range(2)]
        qrt = A.t("qrt", [128, 512], F32)
        NG = 2
        wst4 = [A.t("wst4_%d" % i, [128, DFF], F32) for i in range(NG)]
        wbo = [A.t("wbo%d" % i, [128, DFF], BF16) for i in range(2)]
        ld(qrt[:], c_qr, "c11", ["qrt"])
        ld(qrel[:], c_qrel, "c1", ["qrel"])
        ld(alL[:], c_alL.rearrange("p (h k) -> p h k", h=NH), "c2", ["alL"])
        ld(alR[:], c_alR.rearrange("p (h k) -> p h k", h=NH), "c3", ["alR"])
        ld(btab[:], c_bt.rearrange("p (h k) -> p h k", h=NH), "c4", ["btab"])
        ld(dtab[:], c_dt.rearrange("p (o q) -> p o q", o=4), "c5", ["dtab"])
        def keep(h, i, j):
            dmin = max(0, j * 128 - (i * 512 + 511), i * 512 - (j * 128 + 127))
            return ms[h] * dmin < SKIP_BIAS
        groups = [(h, i, j) for h in range(NH) for i in range(NS) for j in range(NT) if keep(h, i, j)]
        first_j = {}
        last_j = {}
        for (h_, i_, j_) in groups:
            first_j.setdefault((h_, i_), j_)
            last_j[(h_, i_)] = j_
        pending = []

        def emit_scores(n):
            h, i, j = groups[n]
            cch, rb, o = h // 2, (h % 2) * 64, j - 4 * i
            SB = 2 * (n % 2)
            diag = 0 <= o <= 3
            dve_bias = diag or (n % DVE_BIAS_MOD == 0)
            for p in range(2):
                r0 = rb + 32 * p
                op("pe", "matmul", reads=[("KT", j // 4), ("QT", i)], writes=bk(SB + p), out=bank(SB + p),
                   lhsT=KT[r0:r0 + 32, cch, j * 128:(j + 1) * 128], rhs=QT[r0:r0 + 32, cch, i * 512:(i + 1) * 512],
                   start=True, stop=dve_bias, tile_position=(r0, 0))
            if dve_bias and not diag:
                sgn = -ms[h] if o < 0 else ms[h]
                for p in range(2):
                    op("dve", "scalar_tensor_tensor", reads=["qrt"] + bk(SB + p), writes=bk(SB + p), out=bank(SB + p),
                       in0=qrt[:], scalar=sgn, in1=bank(SB + p), op0=ALU.mult, op1=ALU.add)
            elif not diag:
                al = alL if o < 0 else alR
                for p in range(2):
                    r0 = rb + 32 * p
                    op("pe", "matmul", reads=["alL", "alR", "qrel"], writes=bk(SB + p), out=bank(SB + p),
                       lhsT=al[r0:r0 + 2, h, :], rhs=qrel[r0:r0 + 2, :], start=False, stop=True, tile_position=(r0, 0))
            else:
                for p in range(2):
                    op("dve", "scalar_tensor_tensor", reads=["dtab"] + bk(SB + p), writes=bk(SB + p), out=bank(SB + p),
                       in0=dtab[:, o, :], scalar=-ms[h], in1=bank(SB + p), op0=ALU.mult, op1=ALU.add)

        def emit_exp(n):
            h, i, j = groups[n]
            o = j - 4 * i
            sb2 = n % 2
            op("act", "activation", reads=bk(2 * sb2, 2) + ["btab"], writes=[("PT", sb2)], out=PT[sb2][:], in_=bank(2 * sb2, 2),
               func=AF.Exp, bias=btab[:, h, o + NT:o + NT + 1], scale=1.0)

        def emit_pv(n):
            h, i, j = groups[n]
            sb2 = n % 2
            for p in range(2):
                op("pe", "matmul", reads=[("V", j), "Vones", ("PT", sb2)], writes=bk(4 + p),
                   out=ps_all[0:65, (4 + p) * 512:(5 + p) * 512], lhsT=V[:, j, h, :],
                   rhs=PT[sb2][:, p * 512:(p + 1) * 512], start=(j == first_j[(h, i)]), stop=(j == last_j[(h, i)]))

        def epilogue_stages(h, i):
            ys = (h * NS + i) % 2

            def s0():
                op("dve", "tensor_copy", reads=bk(4, 2), writes=["Osb"], out=Osb[0:65, :], in_=ps_all[0:65, 2048:3072])
                op("dve", "reciprocal", reads=["Osb"], writes=["Osb"], out=Osb[64:65, :], in_=Osb[64:65, :])
                op("dve", "tensor_scalar", reads=["Osb", "lamt1"], writes=["Osb"], out=Osb[64:65, 512:1024],
                   in0=Osb[64:65, 512:1024], scalar1=lamt[64:65, 1:2], scalar2=None, op0=ALU.mult)

            def s1():
                for p in range(2):
                    op("pe", "matmul", reads=["Osb", "sel"], writes=bk(6 + p), out=ps_all[0:64, (6 + p) * 512:(7 + p) * 512],
                       lhsT=sel[0:65, 0:64], rhs=Osb[0:65, p * 512:(p + 1) * 512], start=True, stop=True)

            def s2():
                op("dve", "tensor_tensor", reads=["Osb"] + bk(6, 2), writes=["Tm"], out=Tm[:], in0=Osb[0:64, :],
                   in1=ps_all[0:64, 3072:4096], op=ALU.mult)
                op("dve", "tensor_tensor", reads=["Tm"], writes=["att"], out=att[:], in0=Tm[:, 0:512], in1=Tm[:, 512:1024], op=ALU.add)

            def s3():
                op("act", "activation", reads=["att"], writes=["sqb"], out=sqb[:], in_=att[:], func=AF.Square)

            def s4():
                op("pe", "matmul", reads=["sqb", "ones32"], writes=bk(6), out=ps_all[0:64, 3072:3584], lhsT=ones32[0:64, 0:64],
                   rhs=sqb[:], start=True, stop=True)

            def s5():
                rstd_ops(ps_all[0:64, 3072:3584], rsb[:], ("ps", 6), "rsb", 1.0 / 64, npart=64, extra_bias=cst[0:64, 1:2])

            def s6():
                op("dve", "scalar_tensor_tensor", reads=["att", "rsb", "gsubraw"], writes=[("ya", ys)], out=ya[ys][:], in0=att[:],
                   scalar=gsub_t[:, l * 8 + h:l * 8 + h + 1], in1=rsb[:], op0=ALU.mult, op1=ALU.mult)
                st(yts[h * 64:(h + 1) * 64, i * 512:(i + 1) * 512], ya[ys][:], ("ya", ys), [("ya", ys)], [("yts", h // 2)])

            return [s0, s1, s2, s3, s4, s5, s6]

        wgu_v = w_gu[l].rearrange("(j p) n -> p j n", p=128)
        wd_v = w_down[l].rearrange("(f p) n -> p f n", p=128)
        wgub_v = wgu_bf.rearrange("(j p) n -> p j n", p=128)
        wdb_v = wd_bf.rearrange("(f p) n -> p f n", p=128)
        witems = [("gu", j_, half_) for j_ in range(8) for half_ in range(2)] + [("d", f_, 0) for f_ in range(0, 22, 2)]
        wstate = {"k": 0}

        def emit_witem(item):
            k_ = wstate["k"]
            wstate["k"] += 1
            sl, so = k_ % NG, k_ % 2
            if item[0] == "gu":
                _, j_, half_ = item
                ld(wst4[sl][:], wgu_v[:, j_, half_ * DFF:(half_ + 1) * DFF], ("wst4", sl), [("wst4", sl)])
                op("pool", "tensor_copy", reads=[("wst4", sl)], writes=[("wbo", so)], out=wbo[so][:], in_=wst4[sl][:])
                st(wgub_v[:, j_, half_ * DFF:(half_ + 1) * DFF], wbo[so][:], ("wbo", so), [("wbo", so)], [("wgubf", j_)])
            else:
                _, f_, _ = item
                ld(wst4[sl][:, 0:2 * D].rearrange("p (f n) -> p f n", f=2), wd_v[:, f_:f_ + 2, :], ("wst4", sl), [("wst4", sl)])
                op("pool", "tensor_tensor", reads=[("wst4", sl), "GT1"], writes=[("wbo", so)],
                   out=wbo[so][:, 0:2 * D].rearrange("p (f n) -> p f n", f=2),
                   in0=wst4[sl][:, 0:2 * D].rearrange("p (f n) -> p f n", f=2),
                   in1=GT[:, 1:2, :].to_broadcast([128, 2, D]), op=ALU.mult)
                st(wdb_v[:, f_:f_ + 2, :], wbo[so][:, 0:2 * D].rearrange("p (f n) -> p f n", f=2), ("wbo", so), [("wbo", so)],
                   [("wdbf", f_ // 2)])

        wgap = max(1, len(groups) // (len(witems) + 2))
        emit_scores(0)
        for n in range(len(groups)):
            h, i, j = groups[n]
            if witems and n % wgap == wgap - 1:
                emit_witem(witems.pop(0))
            emit_exp(n)
            if n + 1 < len(groups):
                emit_scores(n + 1)
            emit_pv(n)
            if j == last_j[(h, i)]:
                while pending:
                    pending.pop(0)()
                stg_list = epilogue_stages(h, i)
                stg_list[0]()
                pending = stg_list[1:]
            elif pending and (j % EPI_GAP == EPI_GAP - 1 or NT <= 8):
                pending.pop(0)()
        while pending:
            pending.pop(0)()
        while witems:
            emit_witem(witems.pop(0))

        P.barrier()
        A = Alloc(nc, BASE)
        fb = [A.t("fb%d" % i, [128, SP], F32) for i in range(4)]
        ob = A.t("ob", [128, S], BF16)
        wpbd = A.t("wpbd", [128, 128], F32)
        wpb = A.t("wpb", [128, 128], BF16)
        yb = [A.t("yb%d" % i, [128, 512], BF16) for i in range(2)]
        pco = l * 8
        FB = lambda k: ("fb", k)
        for ci in range(2):
            ld(fb[0][:], pc[4 + ci], ("fb", 0), [FB(0)], rkeys=[("pc", 4 + ci), ("pcpad", 4 + ci)])
            ld(fb[1][:], pc[6 + ci], ("fb", 1), [FB(1)], rkeys=[("pc", 6 + ci), ("pcpad", 6 + ci)])
            ld(fb[2][:], pc[2 + ci], ("fb", 2), [FB(2)], rkeys=[("pc", 2 + ci), ("pcpad", 2 + ci)])
            op("dve", "tensor_tensor", reads=[FB(0), FB(1)], writes=[FB(0)], out=fb[0][:], in0=fb[0][:], in1=fb[1][:], op=ALU.mult)
            wc = [pcol_t[:, pco + 2 + jj * 2 + ci:pco + 2 + jj * 2 + ci + 1] for jj in range(3)]
            op("dve", "tensor_scalar", reads=[FB(0), "pcol"], writes=[FB(1)], out=fb[1][:, 8:8 + S], in0=fb[0][:, 7:7 + S],
               scalar1=wc[0], scalar2=None, op0=ALU.mult)
            op("dve", "scalar_tensor_tensor", reads=[FB(0), FB(1), "pcol"], writes=[FB(3)], out=fb[3][:, 8:8 + S],
               in0=fb[0][:, 8:8 + S], scalar=wc[1], in1=fb[1][:, 8:8 + S], op0=ALU.mult, op1=ALU.add)
            op("dve", "scalar_tensor_tensor", reads=[FB(0), FB(3), "pcol"], writes=[FB(1)], out=fb[1][:, 8:8 + S],
               in0=fb[0][:, 9:9 + S], scalar=wc[2], in1=fb[3][:, 8:8 + S], op0=ALU.mult, op1=ALU.add)
            op("dve", "tensor_tensor", reads=[FB(1), FB(2)], writes=["ob"], out=ob[:], in0=fb[1][:, 8:8 + S],
               in1=fb[2][:, 8:8 + S], op=ALU.mult)
            st(yts[768 + ci * 128:768 + (ci + 1) * 128, :], ob[:], "ob", ["ob"], [("yts", 6 + ci)])
        for ci in range(2):
            ld(fb[0][:], pc[ci], ("fb", 0), [FB(0)], rkeys=[("pc", ci), ("pcpad", ci)])
            op("pool", "memset", writes=["wpbd"], ap=wpbd[:], constant=0.0)
            for g2 in range(2):
                ld(wpbd[g2 * 64:(g2 + 1) * 64, g2 * 64:(g2 + 1) * 64], w_pool[l, ci * 2 + g2], ("wp", g2), ["wpbd"])
            op("dve", "tensor_copy", reads=["wpbd"], writes=["wpb"], out=wpb[:], in_=wpbd[:])
            u, p2, p4, p8 = fb[0], fb[1], fb[2], fb[3]
            op("dve", "tensor_tensor", reads=[FB(0)], writes=[FB(1)], out=p2[:, 0:SP - 1], in0=u[:, 0:SP - 1], in1=u[:, 1:SP], op=ALU.add)
            if ci == 0:
                op("dve", "tensor_copy", reads=[FB(1)], writes=[FB(2)], out=fb[2][0:64, 8:8 + S], in_=p2[0:64, 7:7 + S])
                op("dve", "tensor_tensor", reads=[FB(1)], writes=[FB(2)], out=fb[2][64:128, 8:8 + S],
                   in0=p2[64:128, 6:6 + S], in1=p2[64:128, 8:8 + S], op=ALU.add)
                sb_, skey = fb[2], FB(2)
            else:
                op("dve", "tensor_tensor", reads=[FB(1)], writes=[FB(2)], out=p4[:, 0:SP - 3], in0=p2[:, 0:SP - 3],
                   in1=p2[:, 2:SP - 1], op=ALU.add)
                op("dve", "tensor_tensor", reads=[FB(2)], writes=[FB(3)], out=p8[64:128, 0:SP - 7], in0=p4[64:128, 0:SP - 7],
                   in1=p4[64:128, 4:SP - 3], op=ALU.add)
                op("dve", "tensor_tensor", reads=[FB(2), FB(1)], writes=[FB(1)], out=fb[1][0:64, 8:8 + S],
                   in0=p4[0:64, 4:4 + S], in1=p4[0:64, 8:8 + S], op=ALU.add)
                op("dve", "tensor_tensor", reads=[FB(3), FB(1)], writes=[FB(1)], out=fb[1][64:128, 8:8 + S],
                   in0=p8[64:128, 0:S], in1=p8[64:128, 8:8 + S], op=ALU.add)
                sb_, skey = fb[1], FB(1)
            op("dve", "tensor_scalar", reads=[skey, "poolc"], writes=[skey], out=sb_[:, 8:8 + S], in0=sb_[:, 8:8 + S],
               scalar1=poolc[:, ci, 0:1], scalar2=None, op0=ALU.mult)
            op("dve", "tensor_tensor", reads=[skey, "poolc"], writes=[skey], out=sb_[:, 8:16], in0=sb_[:, 8:16],
               in1=poolc[:, ci, 1:9], op=ALU.mult)
            op("dve", "tensor_tensor", reads=[skey, "poolc"], writes=[skey], out=sb_[:, S:S + 8], in0=sb_[:, S:S + 8],
               in1=poolc[:, ci, 9:17], op=ALU.mult)
            op("dve", "tensor_tensor", reads=[skey, FB(0)], writes=["ob"], out=ob[:], in0=sb_[:, 8:8 + S], in1=u[:, 8:8 + S],
               op=ALU.subtract)
            for i in range(NS):
                b = i % 4
                op("pe", "matmul", reads=["wpb", "ob"], writes=bk(b), out=bank(b), lhsT=wpb[:], rhs=ob[:, i * 512:(i + 1) * 512],
                   start=True, stop=True)
                ys = i % 2
                op("act", "activation", reads=bk(b) + ["pcol"], writes=[("yb", ys)], out=yb[ys][:], in_=bank(b),
                   func=AF.Identity, scale=pcol_t[:, pco + ci:pco + ci + 1])
                st(yts[512 + ci * 128:512 + (ci + 1) * 128, i * 512:(i + 1) * 512], yb[ys][:], ("yb", ys), [("yb", ys)],
                   [("yts", 4 + ci)])

        P.barrier()
        A = Alloc(nc, BASE)
        Wo = A.t("Wo", [128, 8, D], BF16)
        wos = [A.t("wos%d" % i, [128, D], F32) for i in range(2)]
        Yt = [A.t("Yt%d" % i, [128, 8, 512], BF16) for i in range(2)]
        xb = [A.t("xb%d" % i, [128, D], F32) for i in range(2)]
        wout_v = w_out[l].rearrange("(j p) n -> p j n", p=128)
        for j in range(8):
            sl = j % 2
            ld(wos[sl][:], wout_v[:, j, :], ("wos", sl), [("wos", sl)])
            op("dve", "tensor_tensor", reads=[("wos", sl), "GT0"], writes=[("Wo", j)], out=Wo[:, j, :], in0=wos[sl][:],
               in1=GT[:, 0, :], op=ALU.mult)
        yts_v = yts.rearrange("(c p) s -> p c s", p=128)
        for i in range(NS):
            ysl = i % 2
            ld(Yt[ysl][:], yts_v[:, :, i * 512:(i + 1) * 512], ("Yt", ysl), [("Yt", ysl)], rkeys=[("yts", c8) for c8 in range(8)])
            for tt in range(4):
                t = i * 4 + tt
                slot = t % 2
                ld(xb[slot][:], x_src[t * 128:(t + 1) * 128, :], ("xb", slot), [("xb", slot)], rkeys=xkeys(t))
                pb = (t % 2) * 2
                for half in range(2):
                    for c8 in range(8):
                        op("pe", "matmul", reads=[("Yt", ysl), ("Wo", c8)], writes=bk(pb + half), out=bank(pb + half),
                           lhsT=Yt[ysl][:, c8, tt * 128:(tt + 1) * 128], rhs=Wo[:, c8, half * 512:(half + 1) * 512],
                           start=(c8 == 0), stop=(c8 == 7))
                op("dve", "tensor_tensor", reads=[("xb", slot)] + bk(pb, 2), writes=[("xb", slot)], out=xb[slot][:],
                   in0=xb[slot][:], in1=bank(pb, 2), op=ALU.add)
                st(out[t * 128:(t + 1) * 128, :], xb[slot][:], ("xbs", slot), [("xb", slot)], [("xo", t)])

        P.barrier()
        A = Alloc(nc, BASE)
        Wgu = A.t("Wgu", [128, 8, 2 * DFF], BF16)
        Wd = A.t("Wd", [128, 22, D], BF16)
        X1 = [A.t("X1_%d" % i, [128, D], F32) for i in range(4)]
        xn2 = A.t("xn2", [128, D], F32)
        hT2 = A.t("hT2", [128, 8, 512], BF16)
        hT3 = A.t("hT3", [128, 8, 512], BF16)
        sgall = A.t("sgall", [128, 2, 512], F32)
        sg = [sgall[:, 0, :], sgall[:, 1, :]]
        fsq = sgall[:].rearrange("p a b -> p (a b)")
        ssq2 = A.t("ssq2", [128, 4], F32)
        A_h = A.off
        Hh = A.t("Hh", [128, 22, 512], BF16)
        for j in range(8):
            ld(Wgu[:, j, :], wgub_v[:, j, :], ("wgl", j % 4), [("Wgu", j)], rkeys=[("wgubf", j)])
        for f in range(0, 22, 2):
            ld(Wd[:, f:f + 2, :], wdb_v[:, f:f + 2, :], ("wdl", (f // 2) % 4), [("Wd", f), ("Wd", f + 1)], rkeys=[("wdbf", f // 2)])
        P.barrier()
        gfb = xn2
        if last:
            A = Alloc(nc, A_h + 22 * 512 * 2)
            gfb = A.t("gfb", [128, D], F32)
            ld(gfb[:], gfin[0:1, :].partition_broadcast(128), "gf", ["gfb"])
        gi = 0
        xa = X1[0:2]
        xr = X1[2:4]
        hTb = [hT2, hT3]

        def load_norm_in(i_, tt_):
            t_ = i_ * 4 + tt_
            ld(xa[t_ % 2][:], out[t_ * 128:(t_ + 1) * 128, :], ("xa", t_ % 2), [("xa", t_ % 2)], rkeys=[("xo", t_)])

        for tt in range(4):
            load_norm_in(0, tt)
            norm_tile(xa[tt % 2], ("xa", tt % 2), xn2, ssq2, hTb[0], tt, 2, 3, 0, hkey=("hT", 0))
        for i in range(NS):
            hcur = hTb[i % 2]
            hk = ("hT", i % 2)
            nxt = i + 1 < NS
            for f in range(22):
                gb = 2 + 2 * (gi % 2)
                sgs = gi % 2
                gi += 1
                for which in range(2):
                    cb = which * DFF + f * 128
                    for j in range(8):
                        op("pe", "matmul", reads=[("Wgu", j), hk], writes=bk(gb + which), out=bank(gb + which),
                           lhsT=Wgu[:, j, cb:cb + 128], rhs=hcur[:, j, :], start=(j == 0), stop=(j == 7))
                op("act", "activation", reads=bk(gb), writes=[("sg", sgs)], out=sg[sgs][:], in_=bank(gb), func=AF.Silu)
                op("dve", "tensor_tensor", reads=[("sg", sgs)] + bk(gb + 1), writes=["Hh"], out=Hh[:, f, :], in0=sg[sgs][:],
                   in1=bank(gb + 1), op=ALU.mult)
                if nxt and f % 5 == 1 and f // 5 < 4:
                    tt_ = f // 5
                    load_norm_in(i + 1, tt_)
                    norm_p1(xa[(i * 4 + 4 + tt_) % 2], ("xa", (i * 4 + 4 + tt_) % 2), xn2, ssq2)
                if nxt and f % 5 == 4 and f // 5 < 4:
                    norm_p2(xn2, hTb[(i + 1) % 2], f // 5, 2, 3, 0, hkey=("hT", (i + 1) % 2))
            for tt in range(4):
                t = i * 4 + tt
                rs = t % 2
                ld(xr[rs][:], out[t * 128:(t + 1) * 128, :], ("xr", rs), [("xr", rs)], rkeys=[("xo", t)])
                for half in range(2):
                    for f in range(22):
                        op("pe", "matmul", reads=["Hh", ("Wd", f)], writes=bk(6 + half), out=bank(6 + half),
                           lhsT=Hh[:, f, tt * 128:(tt + 1) * 128], rhs=Wd[:, f, half * 512:(half + 1) * 512],
                           start=(f == 0), stop=(f == 21))
                op("dve", "tensor_tensor", reads=[("xr", rs)] + bk(6, 2), writes=[("xr", rs)], out=xr[rs][:], in0=xr[rs][:],
                   in1=bank(6, 2), op=ALU.add)
                if last:
                    op("act", "activation", reads=[("xr", rs)], writes=[("sg", 0), ("sg", 1), "ssq2"], out=fsq, in_=xr[rs][:],
                       func=AF.Square, accum_out=ssq2[:, 2:3])
                    rstd_ops(ssq2[:, 2:3], ssq2[:, 3:4], "ssq2", "ssq3", 1.0 / D)
                    op("dve", "scalar_tensor_tensor", reads=[("xr", rs), "ssq3", "gfb"], writes=[("xr", rs)], out=xr[rs][:],
                       in0=xr[rs][:], scalar=ssq2[:, 3:4], in1=gfb[:], op0=ALU.mult, op1=ALU.mult)
                st(out[t * 128:(t + 1) * 128, :], xr[rs][:], ("x1s", rs), [("xr", rs)], [("xo", t)])
    P.emit()
    return nc, P


def make_consts(S):
    NT = S // 128
    NB = 2 * NT
    bf = ml_dtypes.bfloat16
    ms = slopes()
    ident = np.eye(128, dtype=np.float32)
    q = np.arange(512)
    qrel = np.zeros((128, 512), np.float32)
    for r in range(4):
        qrel[32 * r + 0] = q % 256
        qrel[32 * r + 1] = 256 * (q >= 256)
    alL = np.zeros((128, NH, 128), np.float32)
    alR = np.zeros((128, NH, 128), np.float32)
    for h in range(NH):
        rb = (h % 2) * 64
        for p in range(2):
            for rr in range(2):
                alL[rb + 32 * p + rr, h, :] = -ms[h]
                alR[rb + 32 * p + rr, h, :] = ms[h]
    krel = np.arange(128, dtype=np.float64)
    bt = np.zeros((128, NH, NB), np.float64)
    for h in range(NH):
        for o in range(-NT + 1, NT):
            if o < 0:
                bt[:, h, o + NT] = ms[h] * krel + 128.0 * ms[h] * o
            elif o > 3:
                bt[:, h, o + NT] = -ms[h] * krel - 128.0 * ms[h] * o
    dtb = np.zeros((128, 4, 512), np.float32)
    for o in range(4):
        dtb[:, o, :] = np.abs(q[None, :] - krel[:, None] - 128 * o)
    poolc = np.zeros((128, 2, 17), np.float32)
    wins = (2, 4, 8, 16)
    for ci in range(2):
        for g2 in range(2):
            w = wins[ci * 2 + g2]
            rows = slice(g2 * 64, (g2 + 1) * 64)
            poolc[rows, ci, 0] = 1.0 / w
            for e_ in range(8):
                for side, t in ((0, e_), (1, S - 8 + e_)):
                    lo = min(max(t - w // 2, 0), S)
                    hi = min(max(t + w - w // 2, 0), S)
                    poolc[rows, ci, 1 + side * 8 + e_] = float(w) / float(hi - lo)
    return {
        "c_ident": ident,
        "c_qrel": qrel.astype(bf),
        "c_alL": alL.reshape(128, NH * 128).astype(bf),
        "c_alR": alR.reshape(128, NH * 128).astype(bf),
        "c_bt": bt.reshape(128, NH * NB).astype(np.float32),
        "c_dt": dtb.reshape(128, 4 * 512),
        "c_pool": poolc.reshape(128, 34),
        "c_qr": np.ascontiguousarray(np.broadcast_to(q.astype(np.float32)[None, :], (128, 512))),
    }


def col128(v):
    v = np.asarray(v, np.float32)
    return np.ascontiguousarray(v.reshape(-1, 128).T)


def make_in_maps(S, x, c, w_ada, b_ada, g_mix, w_in, lambda_q1, lambda_k1, lambda_q2, lambda_k2,
                 g_subln, w_pool, pool_scale, conv_w, w_out, g_ffn, w_gate_up, w_down, g_final):
    f = lambda a: np.ascontiguousarray(np.asarray(a, np.float32))
    B = x.shape[0]
    consts = make_consts(S)
    gcols = np.concatenate([np.concatenate([col128(g_mix[l]), col128(g_ffn[l])], axis=1) for l in range(DEPTH)], axis=1)
    lam = np.stack([np.concatenate([f(lambda_q1[l]), f(lambda_k1[l]), f(lambda_q2[l]), f(lambda_k2[l])]) for l in range(DEPTH)])
    gsub = np.concatenate([f(g_subln[l]).reshape(NH, 64).T for l in range(DEPTH)], axis=1)
    pcs = []
    for l in range(DEPTH):
        cols = [col128(pool_scale[l])]
        cw = f(conv_w[l])
        for jj in range(3):
            cols.append(col128(cw[jj]))
        pcs.append(np.concatenate(cols, axis=1))
    pcols = np.concatenate(pcs, axis=1)
    shared = {
        "w_ada": f(w_ada), "b_ada": f(b_ada), "gcols": np.ascontiguousarray(gcols), "w_in": f(w_in),
        "lam": np.ascontiguousarray(lam), "gsub": np.ascontiguousarray(gsub), "w_pool": f(w_pool),
        "pcols": np.ascontiguousarray(pcols), "w_out": f(w_out), "w_gate_up": f(w_gate_up), "w_down": f(w_down),
        "gfin": f(g_final).reshape(1, D),
    }
    shared.update(consts)
    maps = []
    xf = f(x)
    cf = f(c)
    for b in range(B):
        m = dict(shared)
        m["x"] = np.ascontiguousarray(xf[b])
        m["cT"] = col128(cf[b])
        maps.append(m)
    return maps


_CACHE = {}


def kernel(**inputs):
    x = np.asarray(inputs["x"])
    B, S, _ = x.shape
    if S not in _CACHE:
        _CACHE[S] = build(S)[0]
    nc = _CACHE[S]
    in_maps = make_in_maps(S, **inputs)
    res = run_bass_kernel_spmd(nc, in_maps, core_ids=list(range(B)))
    return np.stack([np.asarray(r["out"], np.float32) for r in res.results], axis=0)
```
